# Optimizing a Trainium2 kernel written in Bass

```python
import math
import jax
import jax.numpy as jnp
from jax import lax
import numpy as np

D_MODEL = 2048
BATCH = 8
SEQ = 4096
DEPTH = 1

CHUNK = 64
PLE_DIM = 256
MIX_WIDTH = D_MODEL
SSM_WIDTH = MIX_WIDTH // 2
SSM_GROUP = 16
SSM_GROUPS = SSM_WIDTH // SSM_GROUP
SSM_STATE = 64
SB_WIDTH = MIX_WIDTH - SSM_WIDTH
SB_HEAD_DIM = 128
SB_HEADS = SB_WIDTH // SB_HEAD_DIM
Q_BLOCK = 128
D_FF = ((8 * D_MODEL // 3 + 255) // 256) * 256
EPS = 1e-6
DT_MIN = 1e-3
DT_MAX = 1e-1
LAMBDA_RE_MAX = -1e-4

kernel_name = "hybrid_s5_stickbreaking_macaron_layer"


def rms_norm(x, gain):
    xf = x.astype(jnp.float32)
    y = xf * lax.rsqrt(jnp.mean(xf * xf, axis=-1, keepdims=True) + EPS)
    return (y * gain.astype(jnp.float32)).astype(x.dtype)


def swiglu(h, w_gate, w_up, w_down):
    return (jax.nn.silu(h @ w_gate) * (h @ w_up)) @ w_down


def _ssm_combine(left, right):
    a_l, b_l = left
    a_r, b_r = right
    return a_l * a_r, a_r * b_l + b_r


def s5_mixer(u, lam_re, lam_im, b_re, b_im, c_re, c_im, log_dt, d_skip, w_glu, b_glu):
    bsz, seq, _ = u.shape
    f32 = jnp.float32
    lam = lax.complex(jnp.minimum(lam_re.astype(f32), LAMBDA_RE_MAX), lam_im.astype(f32))
    dt = jnp.exp(log_dt.astype(f32))[:, None]
    lam_bar = jnp.exp(lam * dt)
    b_bar = ((lam_bar - 1.0) / lam)[:, :, None] * lax.complex(b_re.astype(f32), b_im.astype(f32))
    c = lax.complex(c_re.astype(f32), c_im.astype(f32))
    d = d_skip.astype(f32).reshape(SSM_GROUPS, SSM_GROUP)
    n_chunks = seq // CHUNK
    u_c = u.astype(f32).reshape(bsz, n_chunks, CHUNK, SSM_GROUPS, SSM_GROUP).transpose(1, 0, 2, 3, 4)
    a_chunk = jnp.broadcast_to(lam_bar, (bsz, CHUNK, SSM_GROUPS, SSM_STATE))

    def step(state, u_k):
        bu = jnp.einsum("bcgh,gph->bcgp", u_k, b_bar)
        a_cum, s_in = lax.associative_scan(_ssm_combine, (a_chunk, bu), axis=1)
        s = s_in + a_cum * state[:, None]
        y = jnp.real(jnp.einsum("bcgp,ghp->bcgh", s, c)) + d * u_k
        return s[:, -1], y

    state0 = jnp.zeros((bsz, SSM_GROUPS, SSM_STATE), jnp.complex64)
    _, y = lax.scan(step, state0, u_c)
    y = y.transpose(1, 0, 2, 3, 4).reshape(bsz, seq, SSM_WIDTH)
    z = jax.nn.gelu(y)
    out = z * jax.nn.sigmoid(z @ w_glu.astype(f32) + b_glu.astype(f32))
    return out.astype(u.dtype)


def stick_breaking_attention(q, k, v):
    f32 = jnp.float32
    seq = q.shape[2]
    scale = SB_HEAD_DIM ** -0.5
    qf, kf, vf = q.astype(f32), k.astype(f32), v.astype(f32)
    outs = []
    for blk in range(seq // Q_BLOCK):
        q0 = blk * Q_BLOCK
        k_end = q0 + Q_BLOCK
        z = jnp.einsum("bhqd,bhkd->bhqk", qf[:, :, q0:k_end], kf[:, :, :k_end]) * scale
        q_pos = q0 + jnp.arange(Q_BLOCK)[:, None]
        k_pos = jnp.arange(k_end)[None, :]
        before = k_pos < q_pos
        log_keep = jnp.where(before, jax.nn.log_sigmoid(-z), 0.0)
        log_pass = lax.cumsum(log_keep, axis=3, reverse=True) - log_keep
        w = jnp.where(before, jnp.exp(jax.nn.log_sigmoid(z) + log_pass), 0.0)
        outs.append(jnp.einsum("bhqk,bhkd->bhqd", w, vf[:, :, :k_end]))
    return jnp.concatenate(outs, axis=2).astype(q.dtype)


def setup_inputs(seed: int = 0) -> dict:
    key = jax.random.key(seed)
    ks = iter(jax.random.split(key, 40))
    D, F = D_MODEL, D_FF
    G, H, P = SSM_GROUPS, SSM_GROUP, SSM_STATE

    def nrm(shape, scale):
        return scale * jax.random.normal(next(ks), shape, jnp.float32)

    def gain(n):
        return 1.0 + nrm((DEPTH, n), 0.02)

    n_idx = jnp.arange(P, dtype=jnp.float32)
    return {
        "x": nrm((BATCH, SEQ, D), 1.0),
        "p": nrm((DEPTH, BATCH, SEQ, PLE_DIM), 1.0),
        "ffn1_norm": gain(D),
        "ffn1_w_gate": nrm((DEPTH, D, F), D ** -0.5),
        "ffn1_w_up": nrm((DEPTH, D, F), D ** -0.5),
        "ffn1_w_down": nrm((DEPTH, F, D), F ** -0.5),
        "mix_norm": gain(D),
        "w_in": nrm((DEPTH, D, SSM_WIDTH + 3 * SB_WIDTH), D ** -0.5),
        "ssm_lambda_re": -0.5 + nrm((DEPTH, G, P), 0.01),
        "ssm_lambda_im": math.pi * jnp.broadcast_to(n_idx, (DEPTH, G, P)) + nrm((DEPTH, G, P), 0.01),
        "ssm_b_re": nrm((DEPTH, G, P, H), (2 * H) ** -0.5),
        "ssm_b_im": nrm((DEPTH, G, P, H), (2 * H) ** -0.5),
        "ssm_c_re": nrm((DEPTH, G, H, P), (2 * P) ** -0.5),
        "ssm_c_im": nrm((DEPTH, G, H, P), (2 * P) ** -0.5),
        "ssm_log_dt": jax.random.uniform(next(ks), (DEPTH, G), jnp.float32, math.log(DT_MIN), math.log(DT_MAX)),
        "ssm_d": nrm((DEPTH, SSM_WIDTH), 1.0),
        "ssm_w_glu": nrm((DEPTH, SSM_WIDTH, SSM_WIDTH), SSM_WIDTH ** -0.5),
        "ssm_b_glu": nrm((DEPTH, SSM_WIDTH), 0.01),
        "q_norm": gain(SB_HEAD_DIM),
        "k_norm": gain(SB_HEAD_DIM),
        "out_norm_ssm": gain(SSM_WIDTH),
        "out_norm_sb": gain(SB_WIDTH),
        "w_out": nrm((DEPTH, MIX_WIDTH, D), MIX_WIDTH ** -0.5),
        "ffn2_norm": gain(D),
        "ffn2_w_gate": nrm((DEPTH, D, F), D ** -0.5),
        "ffn2_w_up": nrm((DEPTH, D, F), D ** -0.5),
        "ffn2_w_down": nrm((DEPTH, F, D), F ** -0.5),
        "ple_norm": gain(D),
        "w_ple_gate": nrm((DEPTH, D, D), D ** -0.5),
        "w_ple_proj": nrm((DEPTH, PLE_DIM, D), PLE_DIM ** -0.5),
        "ple_post_norm": gain(D),
    }


def reference(x, p, ffn1_norm, ffn1_w_gate, ffn1_w_up, ffn1_w_down, mix_norm, w_in,
              ssm_lambda_re, ssm_lambda_im, ssm_b_re, ssm_b_im, ssm_c_re, ssm_c_im,
              ssm_log_dt, ssm_d, ssm_w_glu, ssm_b_glu, q_norm, k_norm,
              out_norm_ssm, out_norm_sb, w_out, ffn2_norm, ffn2_w_gate, ffn2_w_up,
              ffn2_w_down, ple_norm, w_ple_gate, w_ple_proj, ple_post_norm):
    bsz, seq, _ = x.shape
    for i in range(DEPTH):
        x = x + 0.5 * swiglu(rms_norm(x, ffn1_norm[i]), ffn1_w_gate[i], ffn1_w_up[i], ffn1_w_down[i])

        h = rms_norm(x, mix_norm[i])
        proj = h @ w_in[i]
        u = proj[..., :SSM_WIDTH]
        q = proj[..., SSM_WIDTH:SSM_WIDTH + SB_WIDTH].reshape(bsz, seq, SB_HEADS, SB_HEAD_DIM)
        k = proj[..., SSM_WIDTH + SB_WIDTH:SSM_WIDTH + 2 * SB_WIDTH].reshape(bsz, seq, SB_HEADS, SB_HEAD_DIM)
        v = proj[..., SSM_WIDTH + 2 * SB_WIDTH:].reshape(bsz, seq, SB_HEADS, SB_HEAD_DIM)

        y_ssm = s5_mixer(u, ssm_lambda_re[i], ssm_lambda_im[i], ssm_b_re[i], ssm_b_im[i],
                         ssm_c_re[i], ssm_c_im[i], ssm_log_dt[i], ssm_d[i], ssm_w_glu[i], ssm_b_glu[i])

        q = rms_norm(q, q_norm[i]).transpose(0, 2, 1, 3)
        k = rms_norm(k, k_norm[i]).transpose(0, 2, 1, 3)
        y_sb = stick_breaking_attention(q, k, v.transpose(0, 2, 1, 3))
        y_sb = y_sb.transpose(0, 2, 1, 3).reshape(bsz, seq, SB_WIDTH)

        mixed = jnp.concatenate([rms_norm(y_ssm, out_norm_ssm[i]), rms_norm(y_sb, out_norm_sb[i])], axis=-1)
        x = x + mixed @ w_out[i]

        x = x + 0.5 * swiglu(rms_norm(x, ffn2_norm[i]), ffn2_w_gate[i], ffn2_w_up[i], ffn2_w_down[i])

        gate = jax.nn.sigmoid(rms_norm(x, ple_norm[i]) @ w_ple_gate[i])
        e = (p[i] @ w_ple_proj[i]) * gate
        x = x + rms_norm(e, ple_post_norm[i])
    return x
```

```python
import math
from contextlib import ExitStack

import numpy as np

import concourse.bass as bass
import concourse.mybir as mybir
from concourse.bass_utils import run_bass_kernel_spmd

F32 = mybir.dt.float32
BF16 = mybir.dt.bfloat16
I32 = mybir.dt.int32
AF = mybir.ActivationFunctionType
ALU = mybir.AluOpType

D = 2048
DFF = 5632
T = 512
KC = D // 128
FC = DFF // 128
NH = 8
EPS = 1e-6
TWO_PI = 2.0 * math.pi
SHIFT = 16.0 * math.pi

ENGS = ["pe", "act", "dve", "pool", "sp"]
NQ = 8


class Tile:
    __slots__ = ("lw", "rd", "rdd", "pend")

    def __init__(self):
        self.lw = None
        self.rd = {}
        self.rdd = set()
        self.pend = set()


class V:
    __slots__ = ("ap", "ts")

    def __init__(self, ap, ts=None):
        self.ap = ap
        self.ts = [Tile()] if ts is None else ts

    def __call__(self, ap):
        return V(ap, self.ts)


class Prog:
    def __init__(self):
        self.ins = {e: [] for e in ENGS}

    def op(self, eng, fn, r=(), w=(), dma=False):
        idx = len(self.ins[eng])
        deps = set()
        for v in r:
            for t in v.ts:
                if t.lw is not None:
                    deps.add(t.lw)
        for v in w:
            for t in v.ts:
                if t.lw is not None:
                    deps.add(t.lw)
                deps.update(t.rd.items())
                deps.update(t.rdd)
                if t.pend:
                    deps |= t.pend
                    t.pend = set()
        if eng == "pe":
            deps = {d for d in deps if d[0] != "pe"}
        self.ins[eng].append([fn, deps, dma, False])
        for v in r:
            for t in v.ts:
                if dma:
                    t.rdd.add((eng, idx))
                else:
                    t.rd[eng] = idx
        for v in w:
            for t in v.ts:
                t.lw = (eng, idx)
                t.rd = {}
                t.rdd = set()
        return (eng, idx)

    def handoff(self, old, new):
        pend = set()
        for v in old:
            for t in v.ts:
                if t.lw is not None:
                    pend.add(t.lw)
                pend.update(t.rd.items())
                pend.update(t.rdd)
        for v in new:
            for t in v.ts:
                t.pend |= pend

    def emit(self, nc, block, sems, dsems):
        ins = self.ins
        for eng in ENGS:
            for rec in ins[eng]:
                for (e, i) in rec[1]:
                    ins[e][i][3] = True
        cnt = {}
        for eng in ENGS:
            c = 0
            k = 0
            for i, rec in enumerate(ins[eng]):
                if rec[2]:
                    cnt[(eng, i)] = (("d", eng, k % NQ), 16 * (k // NQ + 1))
                    k += 1
                elif rec[3]:
                    c += 1
                    cnt[(eng, i)] = (("c", eng), c)

        def semof(key):
            return dsems[key[1]][key[2]] if key[0] == "d" else sems[key[1]]

        def run(e, eng):
            seen = {}

            def wait(key, val):
                if seen.get(key, 0) < val:
                    e.wait_ge(semof(key), val)
                    seen[key] = val

            k = 0
            for i, rec in enumerate(ins[eng]):
                need = {}
                for d in rec[1]:
                    key, val = cnt[d]
                    if need.get(key, 0) < val:
                        need[key] = val
                for key, val in need.items():
                    wait(key, val)
                if rec[2]:
                    key = ("d", eng, k % NQ)
                    prev = 16 * (k // NQ)
                    if prev > 0:
                        wait(key, prev)
                    rec[0](e).then_inc(semof(key), 16)
                    k += 1
                else:
                    x = rec[0](e)
                    if rec[3]:
                        x.then_inc(sems[eng], 1)
            for j in range(min(k, NQ)):
                n_uses = (k - 1 - j) // NQ + 1
                wait(("d", eng, j), 16 * n_uses)

        @block.tensor
        def _(e):
            run(e, "pe")

        @block.scalar
        def _(e):
            run(e, "act")

        @block.vector
        def _(e):
            run(e, "dve")

        @block.gpsimd
        def _(e):
            run(e, "pool")

        @block.sync
        def _(e):
            run(e, "sp")


WNAMES = [
    ("ffn1_w_gate", D, DFF), ("ffn1_w_up", D, DFF), ("ffn1_w_down", DFF, D),
    ("w_in", D, 4096), ("ssm_w_glu", 1024, 1024), ("w_out", D, D),
    ("ffn2_w_gate", D, DFF), ("ffn2_w_up", D, DFF), ("ffn2_w_down", DFF, D),
    ("w_ple_gate", D, D), ("w_ple_proj", 256, D),
]


def make_blocks():
    blocks = []
    idx = {}

    def add(name, kc0, nk, n0, nb):
        idx.setdefault(name, []).append(len(blocks))
        blocks.append((name, kc0, nk, n0, nb))

    for name, K, N in WNAMES:
        if K == D:
            for oc in range(N // 128):
                add(name, 0, 16, oc * 128, 128)
        elif K == DFF:
            for oc in range(N // 128):
                for kc0, nk in ((0, 16), (16, 16), (32, 12)):
                    add(name, kc0, nk, oc * 128, 128)
        elif K == 1024:
            for ob in range(N // 256):
                add(name, 0, 8, ob * 256, 256)
        elif K == 256:
            for ob in range(N // 1024):
                add(name, 0, 2, ob * 1024, 1024)
    return blocks, idx


SMALL_INPUTS = [
    ("g_ffn1", [128, 16]), ("g_mix", [128, 16]), ("g_ffn2", [128, 16]), ("g_ple", [128, 16]),
    ("g_post", [128, 16]), ("g_ossm", [128, 8]), ("g_osb", [128, 8]),
    ("g_q", [128, 1]), ("g_k", [128, 1]), ("b_glu", [128, 8]), ("d_s", [128, 8]),
    ("lamre_s", [128, 32]), ("lamim_s", [128, 32]), ("logdt_s", [128, 32]),
    ("lamre_b", [128, 4096]), ("lamim_b", [128, 4096]), ("logdt_b", [128, 4096]),
    ("bh_re", [128, 4096]), ("bh_im", [128, 4096]), ("cp_re", [128, 4096]), ("cp_im", [128, 4096]),
    ("c_negtri", [128, 128]), ("c_ident", [128, 128]), ("c_masks", [128, 4 * 512]), ("c_iota", [128, 512]),
]


DBG = False
_LAST = {}


def build(L):
    NT = L // T
    NBLK = L // 128
    blocks, bidx = make_blocks()
    nc = bass.Bass("TRN2", target_bir_lowering=False)

    def din(name, shape, dt=F32):
        return nc.dram_tensor(name, list(shape), dt, kind="ExternalInput").ap()

    xT_d = din("xT", [D, L])
    pT_d = din("pT", [256, L])
    wall_d = din("wall", [len(blocks), 128, 2048])
    sm = {n: din(n, shp) for n, shp in SMALL_INPUTS}
    yT_d = nc.dram_tensor("yT", [D, L], F32, kind="ExternalOutput").ap()
    wscr = nc.dram_tensor("wscr", [len(blocks), 128, 2048], BF16, kind="Internal").ap()
    tabs_d = nc.dram_tensor("tabs", [32, 128, 1024], F32, kind="Internal").ap()
    scon_d = nc.dram_tensor("scon", [32, 128, 640], BF16, kind="Internal").ap()
    kcache = nc.dram_tensor("kcache", [NH, 128, L], BF16, kind="Internal").ap()
    vcache = nc.dram_tensor("vcache", [128, NH, NBLK, 128], BF16, kind="Internal").ap()

    P = Prog()
    es = ExitStack()
    dbg_t = {}

    def dump(name, views, i, dt=F32):
        if not DBG:
            return
        w_ = views[0].ap.shape[-1]
        if name not in dbg_t:
            dbg_t[name] = nc.dram_tensor("dbg_" + name, [NT, 128, len(views) * w_], dt, kind="ExternalOutput").ap()
        for c, v in enumerate(views):
            P.op("sp", lambda e, c=c, v=v: e.dma_start(out=dbg_t[name][i, :, c * w_:(c + 1) * w_], in_=v.ap),
                 r=[v], w=[V(None)], dma=True)

    def sb(name, shape, dt):
        return es.enter_context(nc.sbuf_tensor(name, list(shape), dt))

    xT_t = sb("xT_sb", [128, KC * T], F32)
    hT_t = sb("hT_sb", [128, KC * T], BF16)
    R_t = sb("R_sb", [128, FC * T], BF16)
    yy_t = sb("yy_sb", [128, 8 * T], F32)
    NR = 8
    ring_t = [sb(f"ring{i}", [128, 2048], BF16) for i in range(NR)]
    NF = 6
    tf_t = [sb(f"tf{i}", [128, T], F32) for i in range(NF)]
    NB16 = 6
    tb_t = [sb(f"tb{i}", [128, T], BF16) for i in range(NB16)]
    tabs_t = [sb(f"tabr{i}", [128, 1024], F32) for i in range(2)]
    scon_t = [sb(f"sconr{i}", [128, 640], BF16) for i in range(2)]
    ssmf_t = [sb(f"ssmf{i}", [128, T], F32) for i in range(6)]
    ssmd_t = [sb(f"ssmd{i}", [128, T], BF16) for i in range(8)]
    Sbf_t = [sb(f"Sbf{i}", [128, T], BF16) for i in range(2)]
    masks_t = sb("masks", [128, 4 * T], BF16)
    negtri_t = sb("negtri", [128, 128], BF16)
    negones_t = sb("negones", [128, 128], BF16)
    ones_t = sb("onesb", [128, 128], BF16)
    diagd_t = sb("diagd", [128, 8, 128], BF16)
    gains_t = sb("gains", [128, 128], F32)
    sstate_t = sb("sstate", [128, 16, 32], F32)
    cbias_t = sb("cbias", [128, 8], F32)
    pTb_t = sb("pTb", [128, 2, T], BF16)

    ps_t = [es.enter_context(nc.psum_tensor(f"ps{i}", [128, T], F32)) for i in range(8)]

    sems = {e: es.enter_context(nc.semaphore(f"sem_{e}")) for e in ENGS}
    dsems = {e: [es.enter_context(nc.semaphore(f"dsem_{e}{j}")) for j in range(NQ)] for e in ["sp", "pool"]}
    for e in ENGS:
        dsems.setdefault(e, dsems["sp"])

    xT = [V(xT_t[:, c * T:(c + 1) * T]) for c in range(KC)]
    xT_all = V(xT_t[:].rearrange("p (c t) -> p c t", t=T), [v.ts[0] for v in xT])
    hT = [V(hT_t[:, c * T:(c + 1) * T]) for c in range(KC)]
    PS = [V(ps_t[i][:]) for i in range(8)]
    ring = [V(ring_t[i][:]) for i in range(NR)]
    tf = [V(tf_t[i][:]) for i in range(NF)]
    tb = [V(tb_t[i][:]) for i in range(NB16)]
    cnt = {"ring": 0, "tf": 0, "tb": 0, "tabs": 0, "scon": 0}

    def nxt(kind, lst):
        v = lst[cnt[kind] % len(lst)]
        cnt[kind] += 1
        return v

    def ntf():
        return nxt("tf", tf)

    def ntb():
        return nxt("tb", tb)

    yy = [V(yy_t[:, c * T:(c + 1) * T]) for c in range(8)]
    ssmf = [V(t[:]) for t in ssmf_t]
    ssmd = [V(t[:]) for t in ssmd_t]
    Sbf = [V(Sbf_t[i][:]) for i in range(2)]
    masks = V(masks_t[:])
    negtri = V(negtri_t[:])
    negones = V(negones_t[:])
    onesb = V(ones_t[:])
    diagd = V(diagd_t[:])
    gains = V(gains_t[:])
    sstate = V(sstate_t[:])
    cbias = V(cbias_t[:])
    pTb = V(pTb_t[:])
    tabs_r = [V(tabs_t[i][:]) for i in range(2)]
    scon_r = [V(scon_t[i][:]) for i in range(2)]

    actT = [V(R_t[:, f * T:(f + 1) * T]) for f in range(FC)]
    def rview(off_chunks, n):
        return [V(R_t[:, (off_chunks + i) * T:(off_chunks + i + 1) * T]) for i in range(n)]
    uT = rview(0, 8)
    qT = rview(8, 8)
    kT = rview(16, 8)
    kT_all = V(R_t[:, 16 * T:24 * T], [v.ts[0] for v in kT])
    vtok = [V(R_t[:, 24 * T + tbk * 1024: 24 * T + (tbk + 1) * 1024]) for tbk in range(4)]
    zb = rview(32, 8)
    mixer_views = uT + qT + kT + vtok + zb
    eT = [V(R_t[:, c * 1024:(c + 1) * 1024].bitcast(F32)) for c in range(16)]
    stage32 = [V(R_t[:, k * 4096:(k + 1) * 4096].bitcast(F32)) for k in range(3)]
    stage16 = [V(R_t[:, 12288 + k * 2048: 12288 + (k + 1) * 2048]) for k in range(3)]
    pro_big = [V(R_t[:, k * 2048:(k + 1) * 2048].bitcast(F32)) for k in range(10)]

    wscr_v = [V(wscr[b]) for b in range(len(blocks))]
    tabs_v = [V(tabs_d[q]) for q in range(32)]
    scon_v = [V(scon_d[q]) for q in range(32)]
    kc_v = [[V(kcache[h, :, i * T:(i + 1) * T]) for i in range(NT)] for h in range(NH)]
    vc_v = [V(vcache[:, :, 4 * i:4 * i + 4, :]) for i in range(NT)]
    yout_v = [V(yT_d[:, i * T:(i + 1) * T]) for i in range(NT)]

    def DMA(out, in_, eng="sp"):
        P.op(eng, lambda e: e.dma_start(out=out.ap, in_=in_.ap), r=[in_], w=[out], dma=True)

    def MM(ps, lhsT, rhs, start, stop):
        P.op("pe", lambda e: e.matmul(ps.ap, lhsT=lhsT.ap, rhs=rhs.ap, start=start, stop=stop),
             r=[lhsT, rhs], w=[ps])

    def ACT(out, in_, func, scale=1.0, bias=None):
        r = [in_] + ([bias] if bias is not None else [])
        if bias is None:
            P.op("act", lambda e: e.activation(out=out.ap, in_=in_.ap, func=func, scale=scale), r=r, w=[out])
        else:
            P.op("act", lambda e: e.activation(out=out.ap, in_=in_.ap, func=func, scale=scale, bias=bias.ap),
                 r=r, w=[out])

    def _veng(eng):
        return eng

    def TT(eng, out, a, b, op):
        P.op(eng, lambda e: e.tensor_tensor(out=out.ap, in0=a.ap, in1=b.ap, op=op), r=[a, b], w=[out])

    def TS(eng, out, a, s1, s2, op0, op1=None):
        r = [a] + [s for s in (s1, s2) if isinstance(s, V)]
        s1a = s1.ap if isinstance(s1, V) else s1
        s2a = s2.ap if isinstance(s2, V) else s2
        if op1 is None:
            P.op(eng, lambda e: e.tensor_scalar(out=out.ap, in0=a.ap, scalar1=s1a, scalar2=None, op0=op0), r=r, w=[out])
        else:
            P.op(eng, lambda e: e.tensor_scalar(out=out.ap, in0=a.ap, scalar1=s1a, scalar2=s2a, op0=op0, op1=op1),
                 r=r, w=[out])

    def STT(out, a, s, b, op0, op1):
        r = [a, b] + ([s] if isinstance(s, V) else [])
        sa = s.ap if isinstance(s, V) else s
        P.op("dve", lambda e: e.scalar_tensor_tensor(out=out.ap, in0=a.ap, scalar=sa, in1=b.ap, op0=op0, op1=op1),
             r=r, w=[out])

    def CP(eng, out, in_):
        if eng == "act":
            P.op("act", lambda e: e.copy(out=out.ap, in_=in_.ap), r=[in_], w=[out])
        else:
            P.op(eng, lambda e: e.tensor_copy(out=out.ap, in_=in_.ap), r=[in_], w=[out])

    def MEMSET(eng, out, val):
        P.op(eng, lambda e: e.memset(out.ap, val), w=[out])

    def SCAN(out, d0, d1, init):
        r = [d0, d1] + ([init] if isinstance(init, V) else [])
        ia = init.ap if isinstance(init, V) else init
        P.op("dve", lambda e: e.tensor_tensor_scan(out=out.ap, data0=d0.ap, data1=d1.ap, initial=ia,
                                                    op0=ALU.mult, op1=ALU.add), r=r, w=[out])

    GOFF = {"g_ffn1": 0, "g_mix": 16, "g_ffn2": 32, "g_ple": 48, "g_post": 64, "g_ossm": 80, "g_osb": 88,
            "g_q": 96, "g_k": 97, "b_glu": 98, "d_s": 106}

    def gcol(name, c):
        o = GOFF[name] + c
        return gains(gains_t[:, o:o + 1])

    def cb(i):
        return cbias(cbias_t[:, i:i + 1])

    def ss(i):
        return sstate(sstate_t[:, i, :])

    def ssc(i, q):
        return sstate(sstate_t[:, i, q:q + 1])

    for name, shp in SMALL_INPUTS:
        if name in GOFF:
            o = GOFF[name]
            DMA(gains(gains_t[:, o:o + shp[1]]), V(sm[name]))
    MEMSET("dve", cb(0), EPS)
    MEMSET("dve", cb(1), 1.0)
    MEMSET("dve", cb(2), math.pi)
    MEMSET("dve", cb(3), 0.0)
    MEMSET("dve", onesb, 1.0)
    MEMSET("dve", negones, -1.0)
    TS("dve", gcol("g_q", 0), gcol("g_q", 0), 128.0 ** -0.5, None, ALU.mult)

    big = pro_big
    DMA(big[0](big[0].ap[:, 0:128]), V(sm["c_negtri"]))
    CP("dve", negtri, big[0](big[0].ap[:, 0:128]))
    DMA(big[1](big[1].ap[:, 0:128]), V(sm["c_ident"]))
    for c in range(8):
        TS("dve", diagd(diagd_t[:, c, :]), big[1](big[1].ap[:, 0:128]), gcol("d_s", c), None, ALU.mult)
    for k in range(2):
        DMA(big[2 + k], V(sm["c_masks"][:, k * 1024:(k + 1) * 1024]))
        CP("dve", masks(masks_t[:, k * 1024:(k + 1) * 1024]), big[2 + k])
    iota = big[4](big[4].ap[:, 0:512])
    DMA(iota, V(sm["c_iota"]))

    def sincos(eng_tmp, ang, out_sin, out_cos, shape_ap):
        t_a, t_k, t_i = eng_tmp
        for (o, extra) in ((out_sin, SHIFT), (out_cos, SHIFT + math.pi / 2)):
            TS("dve", t_a, ang, extra, None, ALU.add)
            TS("dve", t_k, t_a, 1.0 / TWO_PI, None, ALU.mult)
            CP("dve", t_i, t_k)
            CP("dve", t_k, t_i)
            STT(t_a, t_k, -TWO_PI, t_a, ALU.mult, ALU.add)
            TS("dve", t_k, t_a, math.pi, TWO_PI, ALU.is_gt, ALU.mult)
            TT("dve", t_a, t_a, t_k, ALU.subtract)
            TS("dve", t_a, t_a, math.pi, -math.pi, ALU.min, ALU.max)
            ACT(o, t_a, AF.Sin)

    def lam_stuff(lr, li, ldt, n, tmp, outs):
        dt_, lrc, a_, b_, sb_, cb_, t0, t1, t2, t3, t4, ti = tmp
        ACT(dt_, ldt, AF.Exp)
        TS("dve", lrc, lr, -1e-4, None, ALU.min)
        TT("dve", a_, lrc, dt_, ALU.mult)
        TT("dve", b_, li, dt_, ALU.mult)
        ACT(outs["ea"], a_, AF.Exp)
        if "theta" in outs:
            CP("dve", outs["theta"], b_)
        sincos((t0, t1, ti), b_, sb_, cb_, None)
        if "coef_re" in outs:
            ea = outs["ea"]
            TT("dve", t0, ea, cb_, ALU.mult)
            TS("dve", t0, t0, -1.0, None, ALU.add)
            TT("dve", t1, ea, sb_, ALU.mult)
            TT("dve", t2, t0, lrc, ALU.mult)
            TT("dve", t3, t1, li, ALU.mult)
            TT("dve", t2, t2, t3, ALU.add)
            TT("dve", t3, t1, lrc, ALU.mult)
            TT("dve", t4, t0, li, ALU.mult)
            TT("dve", t3, t3, t4, ALU.subtract)
            TT("dve", t0, lrc, lrc, ALU.mult)
            TT("dve", t1, li, li, ALU.mult)
            TT("dve", t0, t0, t1, ALU.add)
            P.op("dve", lambda e, o=t0: e.reciprocal(out=o.ap, in_=o.ap), r=[t0], w=[t0])
            TT("dve", outs["coef_re"], t2, t0, ALU.mult)
            TT("dve", outs["coef_im"], t3, t0, ALU.mult)

    st_in = [ss(7), ss(8), ss(9)]
    DMA(st_in[0], V(sm["lamre_s"]))
    DMA(st_in[1], V(sm["lamim_s"]))
    DMA(st_in[2], V(sm["logdt_s"]))
    sm_tmp_t = sb("sm_tmp", [128, 12, 32], F32)
    sm_tmp = V(sm_tmp_t[:])
    sm_ti_t = sb("sm_ti", [128, 32], I32)
    sm_ti = V(sm_ti_t[:])
    tmpl = [sm_tmp(sm_tmp_t[:, i, :]) for i in range(11)] + [sm_ti]
    theta = ss(10)
    lam_stuff(st_in[0], st_in[1], st_in[2], 32, tmpl, {"ea": ss(0), "theta": theta})
    a512 = sm_tmp(sm_tmp_t[:, 11, :])
    TS("dve", a512, theta, float(T), None, ALU.mult)
    sincos((tmpl[6], tmpl[7], sm_ti), a512, ss(2), ss(1), None)
    MEMSET("dve", ss(3), 0.0)
    MEMSET("dve", ss(4), 0.0)

    bigi = V(pTb_t[:].rearrange("p a b -> p (a b)").bitcast(I32))
    for q in range(32):
        ang = big[5](big[5].ap[:, 0:512])
        TS("dve", ang, iota, ssc(10, q), None, ALU.mult)
        tab = big[6 + (q % 2)]
        sincos((big[8](big[8].ap[:, 0:512]), big[9](big[9].ap[:, 0:512]), bigi), ang,
               tab(tab.ap[:, 512:1024]), tab(tab.ap[:, 0:512]), None)
        DMA(tabs_v[q], tab)

    bigi2 = V(hT_t[:, 2048:4096].bitcast(I32))
    bl = ([V(yy_t[:, k * 1024:(k + 1) * 1024]) for k in range(4)] + [V(ring_t[k][:].bitcast(F32)) for k in range(4)]
          + [V(hT_t[:, 0:2048].bitcast(F32))])
    sconst_ap = xT_t[:, 0:2560].bitcast(BF16).rearrange("p (a b) -> p a b", b=640)
    sconst = V(sconst_ap)
    for ch in range(4):
        cs = slice(ch * 1024, (ch + 1) * 1024)
        DMA(big[0], V(sm["lamre_b"][:, cs]))
        DMA(big[1], V(sm["lamim_b"][:, cs]))
        DMA(big[2], V(sm["logdt_b"][:, cs]))
        tmpb = [big[3], big[4], big[5], big[6], big[7], big[8], bl[0], bl[1], bl[2], bl[3], bl[4], bigi2]
        outs = {"ea": big[9], "coef_re": bl[5], "coef_im": bl[6]}
        lam_stuff(big[0], big[1], big[2], 1024, tmpb, outs)
        DMA(big[0], V(sm["bh_re"][:, cs]))
        DMA(big[1], V(sm["bh_im"][:, cs]))
        TT("dve", bl[7], big[0], bl[5], ALU.mult)
        TT("dve", bl[8], big[1], bl[6], ALU.mult)
        TT("dve", bl[7], bl[7], bl[8], ALU.subtract)
        TT("dve", bl[8], big[0], bl[6], ALU.mult)
        TT("dve", bl[0], big[1], bl[5], ALU.mult)
        TT("dve", bl[8], bl[8], bl[0], ALU.add)
        DMA(big[2], V(sm["cp_re"][:, cs]))
        DMA(big[3], V(sm["cp_im"][:, cs]))
        for pq in range(8):
            ps_ = slice(pq * 128, (pq + 1) * 128)
            CP("dve", sconst(sconst_ap[:, pq, 0:128]), bl[7](bl[7].ap[:, ps_]))
            CP("dve", sconst(sconst_ap[:, pq, 128:256]), bl[8](bl[8].ap[:, ps_]))
            CP("dve", sconst(sconst_ap[:, pq, 256:384]), big[2](big[2].ap[:, ps_]))
            TS("dve", sconst(sconst_ap[:, pq, 384:512]), big[2](big[2].ap[:, ps_]), -1.0, None, ALU.mult)
            TS("dve", sconst(sconst_ap[:, pq, 512:640]), big[3](big[3].ap[:, ps_]), -1.0, None, ALU.mult)
        for pq in range(8):
            DMA(scon_v[ch * 8 + pq], sconst(sconst_ap[:, pq, :]))
    P.handoff(big, actT)
    P.handoff(bl + [bigi2, sconst], yy + hT + ring + xT)
    P.handoff([bigi], [pTb])

    cur = {"i": 0}

    def load_block(b):
        name, kc0, nk, n0, nb = blocks[b]
        n = nk * nb
        slot = nxt("ring", ring)
        if cur["i"] == 0:
            DMA(slot(slot.ap[:, 0:n]), V(wall_d[b, :, 0:n]), eng="pool")
            DMA(wscr_v[b](wscr[b, :, 0:n]), slot(slot.ap[:, 0:n]))
        else:
            DMA(slot(slot.ap[:, 0:n]), wscr_v[b](wscr[b, :, 0:n]))
        return slot, slot.ap[:, 0:n].rearrange("p (k n) -> p k n", n=nb)

    def rmsnorm_stats(srcs, nchunks, from_psum=False, out=None):
        psn = PS[7]
        for c in range(nchunks):
            sq = ntb()
            ACT(sq, srcs[c], AF.Square)
            MM(psn, onesb, sq, c == 0, c == nchunks - 1)
        lnv = ntf()
        ACT(lnv, psn, AF.Ln, scale=1.0 / (nchunks * 128), bias=cb(0))
        rstd = ntf() if out is None else out
        ACT(rstd, lnv, AF.Exp, scale=-0.5)
        return rstd

    def norm_to_hT(gname):
        rstd = rmsnorm_stats(xT, KC, out=ssmf[0])
        for c in range(KC):
            STT(hT[c], xT[c], gcol(gname, c), rstd, ALU.mult, ALU.mult)

    def ffn(pre, gname):
        for c in range(KC):
            TS("dve", hT[c], xT[c], gcol(gname, c), None, ALU.mult)
        rstd = rmsnorm_stats(xT, KC, out=ssmf[0])
        gb, ub, db = bidx[pre + "_w_gate"], bidx[pre + "_w_up"], bidx[pre + "_w_down"]
        for f in range(FC):
            sg_, wg = load_block(gb[f])
            su_, wu = load_block(ub[f])
            pg = PS[f % 2]
            pu = PS[2 + f % 2]
            for kc in range(KC):
                MM(pg, sg_(wg[:, kc, :]), hT[kc], kc == 0, kc == KC - 1)
            for kc in range(KC):
                MM(pu, su_(wu[:, kc, :]), hT[kc], kc == 0, kc == KC - 1)
            t1 = ntf()
            TT("dve", t1, pg, rstd, ALU.mult)
            s = ntf()
            ACT(s, t1, AF.Silu)
            t2 = ntf()
            TT("dve", t2, pu, rstd, ALU.mult)
            TT("dve", actT[f], t2, s, ALU.mult)
        for d in range(KC):
            pd = PS[4 + d % 2]
            fi = 0
            for sbk in range(3):
                sl_, wv = load_block(db[d * 3 + sbk])
                nk = blocks[db[d * 3 + sbk]][2]
                for k in range(nk):
                    MM(pd, sl_(wv[:, k, :]), actT[fi], fi == 0, fi == FC - 1)
                    fi += 1
            STT(xT[d], pd, 0.5, xT[d], ALU.mult, ALU.add)

    for i in range(NT):
        cur["i"] = i
        cols = slice(i * T, (i + 1) * T)
        for c in range(KC):
            DMA(xT[c], V(xT_d[c * 128:(c + 1) * 128, cols]))
        ptmp = [ntf(), ntf()]
        for k in range(2):
            DMA(ptmp[k], V(pT_d[k * 128:(k + 1) * 128, cols]))
            CP("dve", pTb(pTb_t[:, k, :]), ptmp[k])

        ffn("ffn1", "g_ffn1")

        dump("x1", xT, i)
        P.handoff(actT, mixer_views)
        norm_to_hT("g_mix")
        wb_in = bidx["w_in"]
        for oc in range(24):
            sl_, wv = load_block(wb_in[oc])
            pp = PS[oc % 4]
            for kc in range(KC):
                MM(pp, sl_(wv[:, kc, :]), hT[kc], kc == 0, kc == KC - 1)
            if oc < 8:
                CP("act", uT[oc], pp)
            else:
                h = (oc - 8) % 8
                isq = oc < 16
                sq = ntb()
                ACT(sq, pp, AF.Square)
                MM(PS[6], onesb, sq, True, True)
                lnv = ntf()
                ACT(lnv, PS[6], AF.Ln, scale=1.0 / 128, bias=cb(0))
                rstd = ntf()
                ACT(rstd, lnv, AF.Exp, scale=-0.5)
                dst = qT[h] if isq else kT[h]
                STT(dst, pp, gcol("g_q" if isq else "g_k", 0), rstd, ALU.mult, ALU.mult)
        for h in range(NH):
            DMA(kc_v[h][i], kT[h])
        for vc in range(8):
            sl_, wv = load_block(wb_in[24 + vc])
            for tbk in range(4):
                pv = PS[4 + (tbk % 2)]
                pvv = pv(pv.ap[:, 0:128])
                for kc in range(KC):
                    MM(pvv, hT[kc](hT_t[:, kc * T + tbk * 128: kc * T + (tbk + 1) * 128]), sl_(wv[:, kc, :]), kc == 0, kc == KC - 1)
                CP("act" if tbk % 2 == 0 else "dve", vtok[tbk](vtok[tbk].ap[:, vc * 128:(vc + 1) * 128]), pvv)
        for tbk in range(4):
            DMA(V(vcache[:, :, 4 * i + tbk, :], vc_v[i].ts),
                vtok[tbk](vtok[tbk].ap.rearrange("p (h d) -> p h d", d=128)))

        if DBG == 2:
            dump("u", uT, i, BF16)
            dump("q", qT, i, BF16)
            dump("k", kT, i, BF16)
            dump("v", vtok, i, BF16)
        def ssm_bufs(q):
            tbv = tabs_r[q % 2]
            return dict(sc=scon_r[q % 2], tbv=tbv, COS=tbv(tbv.ap[:, 0:512]), SIN=tbv(tbv.ap[:, 512:1024]),
                        a1=ssmf[0], b1=ssmf[1], a2=ssmf[2 + (q % 2)], b2=ssmf[4 + (q % 2)],
                        d=ssmd[4 * (q % 2):4 * (q % 2) + 4])

        def ssm_F1a(q):
            c = q // 4
            B_ = ssm_bufs(q)
            DMA(B_["sc"], scon_v[q], eng="pool")
            DMA(B_["tbv"], tabs_v[q], eng="pool")
            sc = B_["sc"]
            MM(PS[5], sc(sc.ap[:, 0:128]), uT[c], True, True)
            MM(PS[6], sc(sc.ap[:, 128:256]), uT[c], True, True)
            TT("dve", B_["a1"], PS[5], B_["COS"], ALU.mult)
            TT("dve", B_["a2"], PS[6], B_["SIN"], ALU.mult)

        def ssm_F1b(q):
            B_ = ssm_bufs(q)
            TT("dve", B_["b1"], PS[6], B_["COS"], ALU.mult)
            TT("dve", B_["b2"], PS[5], B_["SIN"], ALU.mult)

        def ssm_F1c(q):
            B_ = ssm_bufs(q)
            TT("dve", B_["a1"], B_["a1"], B_["a2"], ALU.add)
            TT("dve", B_["b1"], B_["b1"], B_["b2"], ALU.subtract)

        def ssm_F2a(q):
            B_ = ssm_bufs(q)
            rbc = sstate(sstate_t[:, 0, q:q + 1].to_broadcast([128, T]))
            a2 = B_["a2"]
            SCAN(a2, rbc, B_["a1"], ssc(3, q))
            CP("dve", ssc(5, q), a2(a2.ap[:, T - 1:T]))
            d1, d2, d3, d4 = B_["d"]
            TT("pool", d1, a2, B_["COS"], ALU.mult)
            TT("pool", d3, a2, B_["SIN"], ALU.mult)

        def ssm_F2b(q):
            B_ = ssm_bufs(q)
            rbc = sstate(sstate_t[:, 0, q:q + 1].to_broadcast([128, T]))
            b2 = B_["b2"]
            SCAN(b2, rbc, B_["b1"], ssc(4, q))
            CP("dve", ssc(6, q), b2(b2.ap[:, T - 1:T]))
            d1, d2, d3, d4 = B_["d"]
            TT("pool", d2, b2, B_["SIN"], ALU.mult)
            TT("pool", d4, b2, B_["COS"], ALU.mult)

        def ssm_B(c, qq):
            q = 4 * c + qq
            py = PS[7]
            sc = scon_r[q % 2]
            d1, d2, d3, d4 = ssmd[4 * (q % 2):4 * (q % 2) + 4]
            cre = sc(sc.ap[:, 256:384])
            ncre = sc(sc.ap[:, 384:512])
            ncim = sc(sc.ap[:, 512:640])
            MM(py, cre, d1, qq == 0, False)
            MM(py, ncre, d2, False, False)
            MM(py, ncim, d3, False, False)
            MM(py, ncim, d4, False, False)
            if qq == 3:
                ssm_chunk_end(c)

        def ssm_chunk_end(c):
            py = PS[7]
            MM(py, diagd(diagd_t[:, c, :]), uT[c], False, True)
            g1, g2 = ssmf[0], ssmf[1]
            ACT(g1, py, AF.Square)
            TS("dve", g1, g1, 0.044715, 1.0, ALU.mult, ALU.add)
            TT("dve", g1, g1, py, ALU.mult)
            ACT(g2, g1, AF.Sigmoid, scale=1.5957691216057308)
            TT("dve", yy[c], g2, py, ALU.mult)
            CP("act", zb[c], yy[c])

        units = []
        for q in range(32):
            units.append(lambda q=q: ssm_F1a(q))
            units.append(lambda q=q: ssm_F1b(q))
            units.append(lambda q=q: ssm_F1c(q))
            units.append(lambda q=q: ssm_F2a(q))
            units.append(lambda q=q: ssm_F2b(q))
            if q >= 1:
                units.append(lambda q=q: ssm_B((q - 1) // 4, (q - 1) % 4))
        units.append(lambda: ssm_B(7, 3))

        nb = 4 * (i + 1)
        items = []
        for h in range(NH):
            for kblk in range(nb - 1, -1, -1):
                items.append((h, kblk))
        st = {}
        kv_slots = {}

        def stage0(n):
            h, kblk = items[n]
            first = kblk == nb - 1
            if first and h not in kv_slots:
                load_kv(h)
            ks, vs = kv_slots[h]
            kk = ks[kblk // 16]
            kb = kk(kk.ap[:, (kblk % 16) * 128:(kblk % 16 + 1) * 128])
            vv = vs[kblk // 16]
            vb = vv(vv.ap[:, (kblk % 16) * 128:(kblk % 16 + 1) * 128])
            pa = PS[n % 2]
            MM(pa, kb, qT[h], True, True)
            e_ = ntf()
            ACT(e_, pa, AF.Exp)
            st[n] = dict(kb=kb, vb=vb, e=e_, dg=kblk - 4 * i, first=first, last=(kblk == 0), h=h)

        def load_kv(h):
            if True:
                nchk = (nb + 15) // 16
                ks, vs = [], []
                for ck in range(nchk):
                    b0, b1 = ck * 16, min(nb, ck * 16 + 16)
                    ksl = nxt("ring", ring)
                    kview = ksl(ksl.ap[:, 0:(b1 - b0) * 128])
                    DMA(kview, V(kcache[h, :, b0 * 128:b1 * 128], [kc_v[h][j].ts[0] for j in range(b0 // 4, (b1 + 3) // 4)]))
                    vsl = nxt("ring", ring)
                    vview = vsl(vsl.ap[:, 0:(b1 - b0) * 128])
                    DMA(vsl(vsl.ap[:, 0:(b1 - b0) * 128].rearrange("p (b d) -> p b d", d=128)),
                        V(vcache[:, h, b0:b1, :], [vc_v[j].ts[0] for j in range(b0 // 4, (b1 + 3) // 4)]))
                    ks.append(kview)
                    vs.append(vview)
                kv_slots[h] = (ks, vs)

        def stage0b(n):
            s_ = st[n]
            spb = ntb()
            ACT(spb, s_["e"], AF.Ln, bias=cb(1))
            dg = s_["dg"]
            if dg >= 0:
                TT("dve", spb, spb, masks(masks_t[:, dg * T:(dg + 1) * T]), ALU.mult)
            s_["spb"] = spb

        def stage1(n):
            s_ = st[n]
            pe_ = PS[2 + n % 2]
            MM(pe_, s_["kb"], qT[s_["h"]], True, False)
            MM(pe_, negtri, s_["spb"], False, s_["first"])
            if not s_["first"]:
                MM(pe_, negones, Sbf[(n + 1) % 2], False, True)
            if not s_["last"]:
                if s_["first"]:
                    CP("dve", Sbf[n % 2], s_["spb"])
                else:
                    TT("dve", Sbf[n % 2], Sbf[(n + 1) % 2], s_["spb"], ALU.add)
            w_ = ntb()
            ACT(w_, pe_, AF.Exp)
            if s_["dg"] >= 0:
                dg = s_["dg"]
                TT("dve", w_, w_, masks(masks_t[:, dg * T:(dg + 1) * T]), ALU.mult)
            s_["w"] = w_

        def stage2(n):
            s_ = st[n]
            h = s_["h"]
            po = PS[4]
            MM(po, s_["vb"], s_["w"], s_["first"], s_["last"])
            if s_["last"]:
                CP("act", hT[8 + h], po)
            del st[n]

        NI = len(items)
        nsteps = NI + 2
        done = 0
        for n in range(nsteps):
            if n < NI:
                stage0(n)
            if 0 <= n - 1 < NI:
                stage1(n - 1)
            if n < NI:
                stage0b(n)
            if 0 <= n - 2 < NI:
                stage2(n - 2)
            if n < NI and n % nb == 2 and (n // nb) + 1 < NH:
                load_kv(n // nb + 1)
            target = ((n + 1) * len(units) + nsteps - 1) // nsteps
            while done < min(target, len(units)):
                units[done]()
                done += 1
        while done < len(units):
            units[done]()
            done += 1

        TT("dve", ss(7), ss(1), ss(5), ALU.mult)
        TT("dve", ss(8), ss(2), ss(6), ALU.mult)
        TT("dve", ss(3), ss(7), ss(8), ALU.subtract)
        TT("dve", ss(7), ss(2), ss(5), ALU.mult)
        TT("dve", ss(8), ss(1), ss(6), ALU.mult)
        TT("dve", ss(4), ss(7), ss(8), ALU.add)
        glb = bidx["ssm_w_glu"]
        for ob in range(4):
            sl_, wv = load_block(glb[ob])
            for o2 in range(2):
                oc = ob * 2 + o2
                pp = PS[oc % 4]
                for kc in range(8):
                    MM(pp, sl_(wv[:, kc, o2 * 128:(o2 + 1) * 128]), zb[kc], kc == 0, kc == 7)
                g = ntf()
                ACT(g, pp, AF.Sigmoid, bias=gcol("b_glu", oc))
                TT("dve", yy[oc], yy[oc], g, ALU.mult)

        dump("yssm", yy, i)
        rs1 = rmsnorm_stats(yy[0:8], 8)
        for c in range(8):
            STT(hT[c], yy[c], gcol("g_ossm", c), rs1, ALU.mult, ALU.mult)
        rs2 = rmsnorm_stats(hT[8:16], 8)
        for c in range(8):
            STT(hT[8 + c], hT[8 + c], gcol("g_osb", c), rs2, ALU.mult, ALU.mult)
        wob = bidx["w_out"]
        for oc in range(KC):
            sl_, wv = load_block(wob[oc])
            pp = PS[oc % 4]
            for kc in range(KC):
                MM(pp, sl_(wv[:, kc, :]), hT[kc], kc == 0, kc == KC - 1)
            TT("dve", xT[oc], pp, xT[oc], ALU.add)

        dump("x2", xT, i)
        P.handoff(mixer_views, actT)
        ffn("ffn2", "g_ffn2")

        dump("x3", xT, i)
        P.handoff(actT, eT)
        norm_to_hT("g_ple")
        pgb = bidx["w_ple_gate"]
        ppb = bidx["w_ple_proj"]
        pslots = []
        for k in range(2):
            dst = tabs_r[k](tabs_t[k][:].bitcast(BF16))
            if i == 0:
                DMA(dst, V(wall_d[ppb[k], :, 0:2048]), eng="pool")
                DMA(wscr_v[ppb[k]](wscr[ppb[k], :, 0:2048]), dst)
            else:
                DMA(dst, wscr_v[ppb[k]](wscr[ppb[k], :, 0:2048]))
            pslots.append((dst, dst.ap.rearrange("p (k n) -> p k n", n=1024)))
        for oc in range(KC):
            sl_, wv = load_block(pgb[oc])
            pg = PS[oc % 2]
            for kc in range(KC):
                MM(pg, sl_(wv[:, kc, :]), hT[kc], kc == 0, kc == KC - 1)
            psl, pwv = pslots[oc // 8]
            pp = PS[2 + oc % 2]
            o8 = oc % 8
            for kc in range(2):
                MM(pp, psl(pwv[:, kc, o8 * 128:(o8 + 1) * 128]), pTb(pTb_t[:, kc, :]), kc == 0, kc == 1)
            g = ntf()
            ACT(g, pg, AF.Sigmoid)
            TT("dve", eT[oc], g, pp, ALU.mult)
        rs = rmsnorm_stats(eT, KC)
        for c in range(KC):
            STT(eT[c], eT[c], gcol("g_post", c), rs, ALU.mult, ALU.mult)
            TT("dve", xT[c], xT[c], eT[c], ALU.add)
        P.handoff(eT, actT)

        for c in range(KC):
            DMA(V(yT_d[c * 128:(c + 1) * 128, cols]), xT[c])

    block = es.enter_context(nc.Block())
    P.emit(nc, block, sems, dsems)
    es.close()
    return nc


def _small_inputs(inp):
    f = np.float32

    def pc(v, n):
        return np.ascontiguousarray(np.asarray(v, f).reshape(n, 128).T)

    out = {}
    out["g_ffn1"] = pc(inp["ffn1_norm"][0], 16)
    out["g_mix"] = pc(inp["mix_norm"][0], 16)
    out["g_ffn2"] = pc(inp["ffn2_norm"][0], 16)
    out["g_ple"] = pc(inp["ple_norm"][0], 16)
    out["g_post"] = pc(inp["ple_post_norm"][0], 16)
    out["g_ossm"] = pc(inp["out_norm_ssm"][0], 8)
    out["g_osb"] = pc(inp["out_norm_sb"][0], 8)
    out["g_q"] = pc(inp["q_norm"][0], 1)
    out["g_k"] = pc(inp["k_norm"][0], 1)
    out["b_glu"] = pc(inp["ssm_b_glu"][0], 8)
    out["d_s"] = pc(inp["ssm_d"][0], 8)
    lre = np.asarray(inp["ssm_lambda_re"][0], f)
    lim = np.asarray(inp["ssm_lambda_im"][0], f)
    ldt = np.repeat(np.asarray(inp["ssm_log_dt"][0], f)[:, None], 64, axis=1)

    def st(a):
        return np.ascontiguousarray(a.reshape(32, 128).T)

    def bc(a):
        return np.ascontiguousarray(np.broadcast_to(a.reshape(1, 4096), (128, 4096)))

    out["lamre_s"], out["lamim_s"], out["logdt_s"] = st(lre), st(lim), st(ldt)
    out["lamre_b"], out["lamim_b"], out["logdt_b"] = bc(lre), bc(lim), bc(ldt)
    bre = np.asarray(inp["ssm_b_re"][0], f)
    bim = np.asarray(inp["ssm_b_im"][0], f)
    cre = np.asarray(inp["ssm_c_re"][0], f)
    cim = np.asarray(inp["ssm_c_im"][0], f)
    bh_re = np.zeros((128, 4096), f)
    bh_im = np.zeros((128, 4096), f)
    cp_re = np.zeros((128, 4096), f)
    cp_im = np.zeros((128, 4096), f)
    for g in range(64):
        q, half = g // 2, g % 2
        rows = slice(16 * (g % 8), 16 * (g % 8) + 16)
        cols_ = slice(q * 128 + 64 * half, q * 128 + 64 * half + 64)
        bh_re[rows, cols_] = bre[g].T
        bh_im[rows, cols_] = bim[g].T
        jr = slice(64 * half, 64 * half + 64)
        cc = slice(q * 128 + 16 * (g % 8), q * 128 + 16 * (g % 8) + 16)
        cp_re[jr, cc] = cre[g].T
        cp_im[jr, cc] = cim[g].T
    out["bh_re"], out["bh_im"], out["cp_re"], out["cp_im"] = bh_re, bh_im, cp_re, cp_im
    s_idx = np.arange(128)[:, None]
    j_idx = np.arange(128)[None, :]
    out["c_negtri"] = np.where(s_idx >= j_idx, -1.0, 0.0).astype(f)
    out["c_ident"] = np.eye(128, dtype=f)
    t_idx = np.arange(512)[None, :]
    out["c_masks"] = np.concatenate(
        [(t_idx > (s_idx + 128 * d)).astype(f) for d in range(4)], axis=1)
    out["c_iota"] = np.ascontiguousarray(np.broadcast_to(np.arange(512, dtype=f)[None, :], (128, 512)))
    return out


def _block_weights(inputs):
    blocks, bidx = make_blocks()
    wall = np.zeros((len(blocks), 128, 2048), np.float32)
    for name, K, N in WNAMES:
        W = np.asarray(inputs[name], np.float32)[0]
        ids = bidx[name]
        if K == D:
            wall[ids[0]:ids[-1] + 1] = W.reshape(16, 128, N // 128, 128).transpose(2, 1, 0, 3).reshape(N // 128, 128, 2048)
        elif K == DFF:
            Wr = W.reshape(FC, 128, N // 128, 128).transpose(2, 1, 0, 3)
            for oc in range(N // 128):
                for s_, (kc0, nk) in enumerate(((0, 16), (16, 16), (32, 12))):
                    wall[ids[oc * 3 + s_], :, 0:nk * 128] = Wr[oc, :, kc0:kc0 + nk, :].reshape(128, nk * 128)
        elif K == 1024:
            wall[ids[0]:ids[-1] + 1] = W.reshape(8, 128, N // 256, 256).transpose(2, 1, 0, 3).reshape(N // 256, 128, 2048)
        elif K == 256:
            wall[ids[0]:ids[-1] + 1] = W.reshape(2, 128, N // 1024, 1024).transpose(2, 1, 0, 3).reshape(N // 1024, 128, 2048)
    return wall


_NC_CACHE = {}


def kernel(**inputs):
    x = np.asarray(inputs["x"], np.float32)
    p = np.asarray(inputs["p"], np.float32)
    B, L, _ = x.shape
    if L not in _NC_CACHE:
        _NC_CACHE[L] = build(L)
    nc = _NC_CACHE[L]
    small = _small_inputs(inputs)
    wts = {"wall": _block_weights(inputs)}
    in_maps = []
    for b in range(B):
        m = {"xT": np.ascontiguousarray(x[b].T), "pT": np.ascontiguousarray(p[0, b].T)}
        m.update(wts)
        m.update(small)
        in_maps.append(m)
    res = run_bass_kernel_spmd(nc, in_maps, core_ids=list(range(B)))
    _LAST["res"] = res.results
    out = np.stack([np.ascontiguousarray(r["yT"].T) for r in res.results], axis=0)
    return out.astype(np.float32)
```

```python
import math
from contextlib import ExitStack

import numpy as np

import concourse.bass as bass
import concourse.mybir as mybir
from concourse.bass_utils import run_bass_kernel_spmd

F32 = mybir.dt.float32
BF16 = mybir.dt.bfloat16
I32 = mybir.dt.int32
AF = mybir.ActivationFunctionType
ALU = mybir.AluOpType

D = 2048
DFF = 5632
T = 512
KC = D // 128
FC = DFF // 128
NH = 8
EPS = 1e-6
TWO_PI = 2.0 * math.pi
SHIFT = 16.0 * math.pi

ENGS = ["pe", "act", "dve", "pool", "sp"]
NQ = 8


class Tile:
    __slots__ = ("lw", "rd", "rdd", "pend")

    def __init__(self):
        self.lw = None
        self.rd = {}
        self.rdd = set()
        self.pend = set()


class V:
    __slots__ = ("ap", "ts")

    def __init__(self, ap, ts=None):
        self.ap = ap
        self.ts = [Tile()] if ts is None else ts

    def __call__(self, ap):
        return V(ap, self.ts)


class Prog:
    def __init__(self):
        self.ins = {e: [] for e in ENGS}

    def op(self, eng, fn, r=(), w=(), dma=False):
        idx = len(self.ins[eng])
        deps = set()
        for v in r:
            for t in v.ts:
                if t.lw is not None:
                    deps.add(t.lw)
        for v in w:
            for t in v.ts:
                if t.lw is not None:
                    deps.add(t.lw)
                deps.update(t.rd.items())
                deps.update(t.rdd)
                if t.pend:
                    deps |= t.pend
                    t.pend = set()
        if eng == "pe":
            deps = {d for d in deps if d[0] != "pe"}
        self.ins[eng].append([fn, deps, dma, False])
        for v in r:
            for t in v.ts:
                if dma:
                    t.rdd.add((eng, idx))
                else:
                    t.rd[eng] = idx
        for v in w:
            for t in v.ts:
                t.lw = (eng, idx)
                t.rd = {}
                t.rdd = set()
        return (eng, idx)

    def handoff(self, old, new):
        pend = set()
        for v in old:
            for t in v.ts:
                if t.lw is not None:
                    pend.add(t.lw)
                pend.update(t.rd.items())
                pend.update(t.rdd)
        for v in new:
            for t in v.ts:
                t.pend |= pend

    def emit(self, nc, block, sems, dsems):
        ins = self.ins
        for eng in ENGS:
            for rec in ins[eng]:
                for (e, i) in rec[1]:
                    ins[e][i][3] = True
        cnt = {}
        for eng in ENGS:
            c = 0
            k = 0
            for i, rec in enumerate(ins[eng]):
                if rec[2]:
                    cnt[(eng, i)] = (("d", eng, k % NQ), 16 * (k // NQ + 1))
                    k += 1
                elif rec[3]:
                    c += 1
                    cnt[(eng, i)] = (("c", eng), c)

        def semof(key):
            return dsems[key[1]][key[2]] if key[0] == "d" else sems[key[1]]

        def run(e, eng):
            seen = {}

            def wait(key, val):
                if seen.get(key, 0) < val:
                    e.wait_ge(semof(key), val)
                    seen[key] = val

            k = 0
            for i, rec in enumerate(ins[eng]):
                need = {}
                for d in rec[1]:
                    key, val = cnt[d]
                    if need.get(key, 0) < val:
                        need[key] = val
                for key, val in need.items():
                    wait(key, val)
                if rec[2]:
                    key = ("d", eng, k % NQ)
                    prev = 16 * (k // NQ)
                    if prev > 0:
                        wait(key, prev)
                    rec[0](e).then_inc(semof(key), 16)
                    k += 1
                else:
                    x = rec[0](e)
                    if rec[3]:
                        x.then_inc(sems[eng], 1)
            for j in range(min(k, NQ)):
                n_uses = (k - 1 - j) // NQ + 1
                wait(("d", eng, j), 16 * n_uses)

        @block.tensor
        def _(e):
            run(e, "pe")

        @block.scalar
        def _(e):
            run(e, "act")

        @block.vector
        def _(e):
            run(e, "dve")

        @block.gpsimd
        def _(e):
            run(e, "pool")

        @block.sync
        def _(e):
            run(e, "sp")


WNAMES = [
    ("ffn1_w_gate", D, DFF), ("ffn1_w_up", D, DFF), ("ffn1_w_down", DFF, D),
    ("w_in", D, 4096), ("ssm_w_glu", 1024, 1024), ("w_out", D, D),
    ("ffn2_w_gate", D, DFF), ("ffn2_w_up", D, DFF), ("ffn2_w_down", DFF, D),
    ("w_ple_gate", D, D), ("w_ple_proj", 256, D),
]


def make_blocks():
    blocks = []
    idx = {}

    def add(name, kc0, nk, n0, nb):
        idx.setdefault(name, []).append(len(blocks))
        blocks.append((name, kc0, nk, n0, nb))

    for name, K, N in WNAMES:
        if K == D:
            for oc in range(N // 128):
                add(name, 0, 16, oc * 128, 128)
        elif K == DFF:
            for oc in range(N // 128):
                for kc0, nk in ((0, 16), (16, 16), (32, 12)):
                    add(name, kc0, nk, oc * 128, 128)
        elif K == 1024:
            for ob in range(N // 256):
                add(name, 0, 8, ob * 256, 256)
        elif K == 256:
            for ob in range(N // 1024):
                add(name, 0, 2, ob * 1024, 1024)
    return blocks, idx


SMALL_INPUTS = [
    ("g_ffn1", [128, 16]), ("g_mix", [128, 16]), ("g_ffn2", [128, 16]), ("g_ple", [128, 16]),
    ("g_post", [128, 16]), ("g_ossm", [128, 8]), ("g_osb", [128, 8]),
    ("g_q", [128, 1]), ("g_k", [128, 1]), ("b_glu", [128, 8]), ("d_s", [128, 8]),
    ("lamre_s", [128, 32]), ("lamim_s", [128, 32]), ("logdt_s", [128, 32]),
    ("lamre_b", [128, 4096]), ("lamim_b", [128, 4096]), ("logdt_b", [128, 4096]),
    ("bh_re", [128, 4096]), ("bh_im", [128, 4096]), ("cp_re", [128, 4096]), ("cp_im", [128, 4096]),
    ("c_negtri", [128, 128]), ("c_ident", [128, 128]), ("c_masks", [128, 4 * 512]), ("c_iota", [128, 512]),
]


DBG = False
_LAST = {}


def build(L):
    NT = L // T
    NBLK = L // 128
    blocks, bidx = make_blocks()
    nc = bass.Bass("TRN2", target_bir_lowering=False)

    def din(name, shape, dt=F32):
        return nc.dram_tensor(name, list(shape), dt, kind="ExternalInput").ap()

    xT_d = din("xT", [D, L])
    pT_d = din("pT", [256, L])
    wall_d = din("wall", [len(blocks), 128, 2048])
    sm = {n: din(n, shp) for n, shp in SMALL_INPUTS}
    yT_d = nc.dram_tensor("yT", [D, L], F32, kind="ExternalOutput").ap()
    wscr = nc.dram_tensor("wscr", [len(blocks), 128, 2048], BF16, kind="Internal").ap()
    tabs_d = nc.dram_tensor("tabs", [32, 128, 1024], F32, kind="Internal").ap()
    scon_d = nc.dram_tensor("scon", [32, 128, 640], BF16, kind="Internal").ap()
    kcache = nc.dram_tensor("kcache", [NH, 128, L], BF16, kind="Internal").ap()
    vcache = nc.dram_tensor("vcache", [128, NH, NBLK, 128], BF16, kind="Internal").ap()

    P = Prog()
    es = ExitStack()
    dbg_t = {}

    def dump(name, views, i, dt=F32):
        if not DBG:
            return
        w_ = views[0].ap.shape[-1]
        if name not in dbg_t:
            dbg_t[name] = nc.dram_tensor("dbg_" + name, [NT, 128, len(views) * w_], dt, kind="ExternalOutput").ap()
        for c, v in enumerate(views):
            P.op("sp", lambda e, c=c, v=v: e.dma_start(out=dbg_t[name][i, :, c * w_:(c + 1) * w_], in_=v.ap),
                 r=[v], w=[V(None)], dma=True)

    def sb(name, shape, dt):
        return es.enter_context(nc.sbuf_tensor(name, list(shape), dt))

    xT_t = sb("xT_sb", [128, KC * T], F32)
    hT_t = sb("hT_sb", [128, KC * T], BF16)
    R_t = sb("R_sb", [128, FC * T], BF16)
    yy_t = sb("yy_sb", [128, 8 * T], F32)
    NR = 8
    ring_t = [sb(f"ring{i}", [128, 2048], BF16) for i in range(NR)]
    NF = 6
    tf_t = [sb(f"tf{i}", [128, T], F32) for i in range(NF)]
    NB16 = 6
    tb_t = [sb(f"tb{i}", [128, T], BF16) for i in range(NB16)]
    tabs_t = [sb(f"tabr{i}", [128, 1024], F32) for i in range(2)]
    scon_t = [sb(f"sconr{i}", [128, 640], BF16) for i in range(2)]
    ssmf_t = [sb(f"ssmf{i}", [128, T], F32) for i in range(6)]
    ssmd_t = [sb(f"ssmd{i}", [128, T], BF16) for i in range(8)]
    Sbf_t = [sb(f"Sbf{i}", [128, T], BF16) for i in range(2)]
    masks_t = sb("masks", [128, 4 * T], BF16)
    negtri_t = sb("negtri", [128, 128], BF16)
    negones_t = sb("negones", [128, 128], BF16)
    ones_t = sb("onesb", [128, 128], BF16)
    diagd_t = sb("diagd", [128, 8, 128], BF16)
    gains_t = sb("gains", [128, 128], F32)
    sstate_t = sb("sstate", [128, 16, 32], F32)
    cbias_t = sb("cbias", [128, 8], F32)
    pTb_t = sb("pTb", [128, 2, T], BF16)

    ps_t = [es.enter_context(nc.psum_tensor(f"ps{i}", [128, T], F32)) for i in range(8)]

    sems = {e: es.enter_context(nc.semaphore(f"sem_{e}")) for e in ENGS}
    dsems = {e: [es.enter_context(nc.semaphore(f"dsem_{e}{j}")) for j in range(NQ)] for e in ["sp", "pool"]}
    for e in ENGS:
        dsems.setdefault(e, dsems["sp"])

    xT = [V(xT_t[:, c * T:(c + 1) * T]) for c in range(KC)]
    xT_all = V(xT_t[:].rearrange("p (c t) -> p c t", t=T), [v.ts[0] for v in xT])
    hT = [V(hT_t[:, c * T:(c + 1) * T]) for c in range(KC)]
    PS = [V(ps_t[i][:]) for i in range(8)]
    ring = [V(ring_t[i][:]) for i in range(NR)]
    tf = [V(tf_t[i][:]) for i in range(NF)]
    tb = [V(tb_t[i][:]) for i in range(NB16)]
    cnt = {"ring": 0, "tf": 0, "tb": 0, "tabs": 0, "scon": 0}

    def nxt(kind, lst):
        v = lst[cnt[kind] % len(lst)]
        cnt[kind] += 1
        return v

    def ntf():
        return nxt("tf", tf)

    def ntb():
        return nxt("tb", tb)

    yy = [V(yy_t[:, c * T:(c + 1) * T]) for c in range(8)]
    ssmf = [V(t[:]) for t in ssmf_t]
    ssmd = [V(t[:]) for t in ssmd_t]
    Sbf = [V(Sbf_t[i][:]) for i in range(2)]
    masks = V(masks_t[:])
    negtri = V(negtri_t[:])
    negones = V(negones_t[:])
    onesb = V(ones_t[:])
    diagd = V(diagd_t[:])
    gains = V(gains_t[:])
    sstate = V(sstate_t[:])
    cbias = V(cbias_t[:])
    pTb = V(pTb_t[:])
    tabs_r = [V(tabs_t[i][:]) for i in range(2)]
    scon_r = [V(scon_t[i][:]) for i in range(2)]

    actT = [V(R_t[:, f * T:(f + 1) * T]) for f in range(FC)]
    def rview(off_chunks, n):
        return [V(R_t[:, (off_chunks + i) * T:(off_chunks + i + 1) * T]) for i in range(n)]
    uT = rview(0, 8)
    qT = rview(8, 8)
    kT = rview(16, 8)
    kT_all = V(R_t[:, 16 * T:24 * T], [v.ts[0] for v in kT])
    vtok = [V(R_t[:, 24 * T + tbk * 1024: 24 * T + (tbk + 1) * 1024]) for tbk in range(4)]
    zb = rview(32, 8)
    mixer_views = uT + qT + kT + vtok + zb
    eT = [V(R_t[:, c * 1024:(c + 1) * 1024].bitcast(F32)) for c in range(16)]
    stage32 = [V(R_t[:, k * 4096:(k + 1) * 4096].bitcast(F32)) for k in range(3)]
    stage16 = [V(R_t[:, 12288 + k * 2048: 12288 + (k + 1) * 2048]) for k in range(3)]
    pro_big = [V(R_t[:, k * 2048:(k + 1) * 2048].bitcast(F32)) for k in range(10)]

    wscr_v = [V(wscr[b]) for b in range(len(blocks))]
    tabs_v = [V(tabs_d[q]) for q in range(32)]
    scon_v = [V(scon_d[q]) for q in range(32)]
    kc_v = [[V(kcache[h, :, i * T:(i + 1) * T]) for i in range(NT)] for h in range(NH)]
    vc_v = [V(vcache[:, :, 4 * i:4 * i + 4, :]) for i in range(NT)]
    yout_v = [V(yT_d[:, i * T:(i + 1) * T]) for i in range(NT)]

    def DMA(out, in_, eng="sp"):
        P.op(eng, lambda e: e.dma_start(out=out.ap, in_=in_.ap), r=[in_], w=[out], dma=True)

    def MM(ps, lhsT, rhs, start, stop):
        P.op("pe", lambda e: e.matmul(ps.ap, lhsT=lhsT.ap, rhs=rhs.ap, start=start, stop=stop),
             r=[lhsT, rhs], w=[ps])

    def ACT(out, in_, func, scale=1.0, bias=None):
        r = [in_] + ([bias] if bias is not None else [])
        if bias is None:
            P.op("act", lambda e: e.activation(out=out.ap, in_=in_.ap, func=func, scale=scale), r=r, w=[out])
        else:
            P.op("act", lambda e: e.activation(out=out.ap, in_=in_.ap, func=func, scale=scale, bias=bias.ap),
                 r=r, w=[out])

    def _veng(eng):
        return eng

    def TT(eng, out, a, b, op):
        P.op(eng, lambda e: e.tensor_tensor(out=out.ap, in0=a.ap, in1=b.ap, op=op), r=[a, b], w=[out])

    def TS(eng, out, a, s1, s2, op0, op1=None):
        r = [a] + [s for s in (s1, s2) if isinstance(s, V)]
        s1a = s1.ap if isinstance(s1, V) else s1
        s2a = s2.ap if isinstance(s2, V) else s2
        if op1 is None:
            P.op(eng, lambda e: e.tensor_scalar(out=out.ap, in0=a.ap, scalar1=s1a, scalar2=None, op0=op0), r=r, w=[out])
        else:
            P.op(eng, lambda e: e.tensor_scalar(out=out.ap, in0=a.ap, scalar1=s1a, scalar2=s2a, op0=op0, op1=op1),
                 r=r, w=[out])

    def STT(out, a, s, b, op0, op1):
        r = [a, b] + ([s] if isinstance(s, V) else [])
        sa = s.ap if isinstance(s, V) else s
        P.op("dve", lambda e: e.scalar_tensor_tensor(out=out.ap, in0=a.ap, scalar=sa, in1=b.ap, op0=op0, op1=op1),
             r=r, w=[out])

    def CP(eng, out, in_):
        if eng == "act":
            P.op("act", lambda e: e.copy(out=out.ap, in_=in_.ap), r=[in_], w=[out])
        else:
            P.op(eng, lambda e: e.tensor_copy(out=out.ap, in_=in_.ap), r=[in_], w=[out])

    def MEMSET(eng, out, val):
        P.op(eng, lambda e: e.memset(out.ap, val), w=[out])

    def SCAN(out, d0, d1, init):
        r = [d0, d1] + ([init] if isinstance(init, V) else [])
        ia = init.ap if isinstance(init, V) else init
        P.op("dve", lambda e: e.tensor_tensor_scan(out=out.ap, data0=d0.ap, data1=d1.ap, initial=ia,
                                                    op0=ALU.mult, op1=ALU.add), r=r, w=[out])

    GOFF = {"g_ffn1": 0, "g_mix": 16, "g_ffn2": 32, "g_ple": 48, "g_post": 64, "g_ossm": 80, "g_osb": 88,
            "g_q": 96, "g_k": 97, "b_glu": 98, "d_s": 106}

    def gcol(name, c):
        o = GOFF[name] + c
        return gains(gains_t[:, o:o + 1])

    def cb(i):
        return cbias(cbias_t[:, i:i + 1])

    def ss(i):
        return sstate(sstate_t[:, i, :])

    def ssc(i, q):
        return sstate(sstate_t[:, i, q:q + 1])

    for name, shp in SMALL_INPUTS:
        if name in GOFF:
            o = GOFF[name]
            DMA(gains(gains_t[:, o:o + shp[1]]), V(sm[name]))
    MEMSET("dve", cb(0), EPS)
    MEMSET("dve", cb(1), 1.0)
    MEMSET("dve", cb(2), math.pi)
    MEMSET("dve", cb(3), 0.0)
    MEMSET("dve", onesb, 1.0)
    MEMSET("dve", negones, -1.0)
    TS("dve", gcol("g_q", 0), gcol("g_q", 0), 128.0 ** -0.5, None, ALU.mult)

    big = pro_big
    DMA(big[0](big[0].ap[:, 0:128]), V(sm["c_negtri"]))
    CP("dve", negtri, big[0](big[0].ap[:, 0:128]))
    DMA(big[1](big[1].ap[:, 0:128]), V(sm["c_ident"]))
    for c in range(8):
        TS("dve", diagd(diagd_t[:, c, :]), big[1](big[1].ap[:, 0:128]), gcol("d_s", c), None, ALU.mult)
    for k in range(2):
        DMA(big[2 + k], V(sm["c_masks"][:, k * 1024:(k + 1) * 1024]))
        CP("dve", masks(masks_t[:, k * 1024:(k + 1) * 1024]), big[2 + k])
    iota = big[4](big[4].ap[:, 0:512])
    DMA(iota, V(sm["c_iota"]))

    def sincos(eng_tmp, ang, out_sin, out_cos, shape_ap):
        t_a, t_k, t_i = eng_tmp
        for (o, extra) in ((out_sin, SHIFT), (out_cos, SHIFT + math.pi / 2)):
            TS("dve", t_a, ang, extra, None, ALU.add)
            TS("dve", t_k, t_a, 1.0 / TWO_PI, None, ALU.mult)
            CP("dve", t_i, t_k)
            CP("dve", t_k, t_i)
            STT(t_a, t_k, -TWO_PI, t_a, ALU.mult, ALU.add)
            TS("dve", t_k, t_a, math.pi, TWO_PI, ALU.is_gt, ALU.mult)
            TT("dve", t_a, t_a, t_k, ALU.subtract)
            TS("dve", t_a, t_a, math.pi, -math.pi, ALU.min, ALU.max)
            ACT(o, t_a, AF.Sin)

    def lam_stuff(lr, li, ldt, n, tmp, outs):
        dt_, lrc, a_, b_, sb_, cb_, t0, t1, t2, t3, t4, ti = tmp
        ACT(dt_, ldt, AF.Exp)
        TS("dve", lrc, lr, -1e-4, None, ALU.min)
        TT("dve", a_, lrc, dt_, ALU.mult)
        TT("dve", b_, li, dt_, ALU.mult)
        ACT(outs["ea"], a_, AF.Exp)
        if "theta" in outs:
            CP("dve", outs["theta"], b_)
        sincos((t0, t1, ti), b_, sb_, cb_, None)
        if "coef_re" in outs:
            ea = outs["ea"]
            TT("dve", t0, ea, cb_, ALU.mult)
            TS("dve", t0, t0, -1.0, None, ALU.add)
            TT("dve", t1, ea, sb_, ALU.mult)
            TT("dve", t2, t0, lrc, ALU.mult)
            TT("dve", t3, t1, li, ALU.mult)
            TT("dve", t2, t2, t3, ALU.add)
            TT("dve", t3, t1, lrc, ALU.mult)
            TT("dve", t4, t0, li, ALU.mult)
            TT("dve", t3, t3, t4, ALU.subtract)
            TT("dve", t0, lrc, lrc, ALU.mult)
            TT("dve", t1, li, li, ALU.mult)
            TT("dve", t0, t0, t1, ALU.add)
            P.op("dve", lambda e, o=t0: e.reciprocal(out=o.ap, in_=o.ap), r=[t0], w=[t0])
            TT("dve", outs["coef_re"], t2, t0, ALU.mult)
            TT("dve", outs["coef_im"], t3, t0, ALU.mult)

    st_in = [ss(7), ss(8), ss(9)]
    DMA(st_in[0], V(sm["lamre_s"]))
    DMA(st_in[1], V(sm["lamim_s"]))
    DMA(st_in[2], V(sm["logdt_s"]))
    sm_tmp_t = sb("sm_tmp", [128, 12, 32], F32)
    sm_tmp = V(sm_tmp_t[:])
    sm_ti_t = sb("sm_ti", [128, 32], I32)
    sm_ti = V(sm_ti_t[:])
    tmpl = [sm_tmp(sm_tmp_t[:, i, :]) for i in range(11)] + [sm_ti]
    theta = ss(10)
    lam_stuff(st_in[0], st_in[1], st_in[2], 32, tmpl, {"ea": ss(0), "theta": theta})
    a512 = sm_tmp(sm_tmp_t[:, 11, :])
    TS("dve", a512, theta, float(T), None, ALU.mult)
    sincos((tmpl[6], tmpl[7], sm_ti), a512, ss(2), ss(1), None)
    MEMSET("dve", ss(3), 0.0)
    MEMSET("dve", ss(4), 0.0)

    bigi = V(pTb_t[:].rearrange("p a b -> p (a b)").bitcast(I32))
    for q in range(32):
        ang = big[5](big[5].ap[:, 0:512])
        TS("dve", ang, iota, ssc(10, q), None, ALU.mult)
        tab = big[6 + (q % 2)]
        sincos((big[8](big[8].ap[:, 0:512]), big[9](big[9].ap[:, 0:512]), bigi), ang,
               tab(tab.ap[:, 512:1024]), tab(tab.ap[:, 0:512]), None)
        DMA(tabs_v[q], tab)

    bigi2 = V(hT_t[:, 2048:4096].bitcast(I32))
    bl = ([V(yy_t[:, k * 1024:(k + 1) * 1024]) for k in range(4)] + [V(ring_t[k][:].bitcast(F32)) for k in range(4)]
          + [V(hT_t[:, 0:2048].bitcast(F32))])
    sconst_ap = xT_t[:, 0:2560].bitcast(BF16).rearrange("p (a b) -> p a b", b=640)
    sconst = V(sconst_ap)
    for ch in range(4):
        cs = slice(ch * 1024, (ch + 1) * 1024)
        DMA(big[0], V(sm["lamre_b"][:, cs]))
        DMA(big[1], V(sm["lamim_b"][:, cs]))
        DMA(big[2], V(sm["logdt_b"][:, cs]))
        tmpb = [big[3], big[4], big[5], big[6], big[7], big[8], bl[0], bl[1], bl[2], bl[3], bl[4], bigi2]
        outs = {"ea": big[9], "coef_re": bl[5], "coef_im": bl[6]}
        lam_stuff(big[0], big[1], big[2], 1024, tmpb, outs)
        DMA(big[0], V(sm["bh_re"][:, cs]))
        DMA(big[1], V(sm["bh_im"][:, cs]))
        TT("dve", bl[7], big[0], bl[5], ALU.mult)
        TT("dve", bl[8], big[1], bl[6], ALU.mult)
        TT("dve", bl[7], bl[7], bl[8], ALU.subtract)
        TT("dve", bl[8], big[0], bl[6], ALU.mult)
        TT("dve", bl[0], big[1], bl[5], ALU.mult)
        TT("dve", bl[8], bl[8], bl[0], ALU.add)
        DMA(big[2], V(sm["cp_re"][:, cs]))
        DMA(big[3], V(sm["cp_im"][:, cs]))
        for pq in range(8):
            ps_ = slice(pq * 128, (pq + 1) * 128)
            CP("dve", sconst(sconst_ap[:, pq, 0:128]), bl[7](bl[7].ap[:, ps_]))
            CP("dve", sconst(sconst_ap[:, pq, 128:256]), bl[8](bl[8].ap[:, ps_]))
            CP("dve", sconst(sconst_ap[:, pq, 256:384]), big[2](big[2].ap[:, ps_]))
            TS("dve", sconst(sconst_ap[:, pq, 384:512]), big[2](big[2].ap[:, ps_]), -1.0, None, ALU.mult)
            TS("dve", sconst(sconst_ap[:, pq, 512:640]), big[3](big[3].ap[:, ps_]), -1.0, None, ALU.mult)
        for pq in range(8):
            DMA(scon_v[ch * 8 + pq], sconst(sconst_ap[:, pq, :]))
    P.handoff(big, actT)
    P.handoff(bl + [bigi2, sconst], yy + hT + ring + xT)
    P.handoff([bigi], [pTb])

    cur = {"i": 0}

    def load_block(b):
        name, kc0, nk, n0, nb = blocks[b]
        n = nk * nb
        slot = nxt("ring", ring)
        if cur["i"] == 0:
            DMA(slot(slot.ap[:, 0:n]), V(wall_d[b, :, 0:n]), eng="pool")
            DMA(wscr_v[b](wscr[b, :, 0:n]), slot(slot.ap[:, 0:n]))
        else:
            DMA(slot(slot.ap[:, 0:n]), wscr_v[b](wscr[b, :, 0:n]))
        return slot, slot.ap[:, 0:n].rearrange("p (k n) -> p k n", n=nb)

    def rmsnorm_stats(srcs, nchunks, from_psum=False, out=None):
        psn = PS[7]
        for c in range(nchunks):
            sq = ntb()
            ACT(sq, srcs[c], AF.Square)
            MM(psn, onesb, sq, c == 0, c == nchunks - 1)
        lnv = ntf()
        ACT(lnv, psn, AF.Ln, scale=1.0 / (nchunks * 128), bias=cb(0))
        rstd = ntf() if out is None else out
        ACT(rstd, lnv, AF.Exp, scale=-0.5)
        return rstd

    def norm_to_hT(gname):
        rstd = rmsnorm_stats(xT, KC, out=ssmf[0])
        for c in range(KC):
            STT(hT[c], xT[c], gcol(gname, c), rstd, ALU.mult, ALU.mult)

    def ffn(pre, gname):
        for c in range(KC):
            TS("dve", hT[c], xT[c], gcol(gname, c), None, ALU.mult)
        rstd = rmsnorm_stats(xT, KC, out=ssmf[0])
        gb, ub, db = bidx[pre + "_w_gate"], bidx[pre + "_w_up"], bidx[pre + "_w_down"]
        for f in range(FC):
            sg_, wg = load_block(gb[f])
            su_, wu = load_block(ub[f])
            pg = PS[f % 2]
            pu = PS[2 + f % 2]
            for kc in range(KC):
                MM(pg, sg_(wg[:, kc, :]), hT[kc], kc == 0, kc == KC - 1)
            for kc in range(KC):
                MM(pu, su_(wu[:, kc, :]), hT[kc], kc == 0, kc == KC - 1)
            t1 = ntf()
            TT("dve", t1, pg, rstd, ALU.mult)
            s = ntf()
            ACT(s, t1, AF.Silu)
            t2 = ntf()
            TT("dve", t2, pu, rstd, ALU.mult)
            TT("dve", actT[f], t2, s, ALU.mult)
        for d in range(KC):
            pd = PS[4 + d % 2]
            fi = 0
            for sbk in range(3):
                sl_, wv = load_block(db[d * 3 + sbk])
                nk = blocks[db[d * 3 + sbk]][2]
                for k in range(nk):
                    MM(pd, sl_(wv[:, k, :]), actT[fi], fi == 0, fi == FC - 1)
                    fi += 1
            STT(xT[d], pd, 0.5, xT[d], ALU.mult, ALU.add)

    for i in range(NT):
        cur["i"] = i
        cols = slice(i * T, (i + 1) * T)
        for c in range(KC):
            DMA(xT[c], V(xT_d[c * 128:(c + 1) * 128, cols]))
        ptmp = [ntf(), ntf()]
        for k in range(2):
            DMA(ptmp[k], V(pT_d[k * 128:(k + 1) * 128, cols]))
            CP("dve", pTb(pTb_t[:, k, :]), ptmp[k])

        ffn("ffn1", "g_ffn1")

        dump("x1", xT, i)
        P.handoff(actT, mixer_views)
        norm_to_hT("g_mix")
        wb_in = bidx["w_in"]
        for oc in range(24):
            sl_, wv = load_block(wb_in[oc])
            pp = PS[oc % 4]
            for kc in range(KC):
                MM(pp, sl_(wv[:, kc, :]), hT[kc], kc == 0, kc == KC - 1)
            if oc < 8:
                CP("act", uT[oc], pp)
            else:
                h = (oc - 8) % 8
                isq = oc < 16
                sq = ntb()
                ACT(sq, pp, AF.Square)
                MM(PS[6], onesb, sq, True, True)
                lnv = ntf()
                ACT(lnv, PS[6], AF.Ln, scale=1.0 / 128, bias=cb(0))
                rstd = ntf()
                ACT(rstd, lnv, AF.Exp, scale=-0.5)
                dst = qT[h] if isq else kT[h]
                STT(dst, pp, gcol("g_q" if isq else "g_k", 0), rstd, ALU.mult, ALU.mult)
        for h in range(NH):
            DMA(kc_v[h][i], kT[h])
        for vc in range(8):
            sl_, wv = load_block(wb_in[24 + vc])
            for tbk in range(4):
                pv = PS[4 + (tbk % 2)]
                pvv = pv(pv.ap[:, 0:128])
                for kc in range(KC):
                    MM(pvv, hT[kc](hT_t[:, kc * T + tbk * 128: kc * T + (tbk + 1) * 128]), sl_(wv[:, kc, :]), kc == 0, kc == KC - 1)
                CP("act" if tbk % 2 == 0 else "dve", vtok[tbk](vtok[tbk].ap[:, vc * 128:(vc + 1) * 128]), pvv)
        for tbk in range(4):
            DMA(V(vcache[:, :, 4 * i + tbk, :], vc_v[i].ts),
                vtok[tbk](vtok[tbk].ap.rearrange("p (h d) -> p h d", d=128)))

        if DBG == 2:
            dump("u", uT, i, BF16)
            dump("q", qT, i, BF16)
            dump("k", kT, i, BF16)
            dump("v", vtok, i, BF16)
        def ssm_bufs(q):
            tbv = tabs_r[q % 2]
            return dict(sc=scon_r[q % 2], tbv=tbv, COS=tbv(tbv.ap[:, 0:512]), SIN=tbv(tbv.ap[:, 512:1024]),
                        a1=ssmf[0], b1=ssmf[1], a2=ssmf[2 + (q % 2)], b2=ssmf[4 + (q % 2)],
                        d=ssmd[4 * (q % 2):4 * (q % 2) + 4])

        def ssm_F1a(q):
            c = q // 4
            B_ = ssm_bufs(q)
            DMA(B_["sc"], scon_v[q])
            DMA(B_["tbv"], tabs_v[q])
            sc = B_["sc"]
            MM(PS[5], sc(sc.ap[:, 0:128]), uT[c], True, True)
            MM(PS[6], sc(sc.ap[:, 128:256]), uT[c], True, True)
            TT("dve", B_["a1"], PS[5], B_["COS"], ALU.mult)
            TT("dve", B_["a2"], PS[6], B_["SIN"], ALU.mult)

        def ssm_F1b(q):
            B_ = ssm_bufs(q)
            TT("dve", B_["b1"], PS[6], B_["COS"], ALU.mult)
            TT("dve", B_["b2"], PS[5], B_["SIN"], ALU.mult)

        def ssm_F1c(q):
            B_ = ssm_bufs(q)
            TT("dve", B_["a1"], B_["a1"], B_["a2"], ALU.add)
            TT("dve", B_["b1"], B_["b1"], B_["b2"], ALU.subtract)

        def ssm_F2a(q):
            B_ = ssm_bufs(q)
            rbc = sstate(sstate_t[:, 0, q:q + 1].to_broadcast([128, T]))
            a2 = B_["a2"]
            SCAN(a2, rbc, B_["a1"], ssc(3, q))
            CP("dve", ssc(5, q), a2(a2.ap[:, T - 1:T]))
            d1, d2, d3, d4 = B_["d"]
            TT("pool", d1, a2, B_["COS"], ALU.mult)
            TT("pool", d3, a2, B_["SIN"], ALU.mult)

        def ssm_F2b(q):
            B_ = ssm_bufs(q)
            rbc = sstate(sstate_t[:, 0, q:q + 1].to_broadcast([128, T]))
            b2 = B_["b2"]
            SCAN(b2, rbc, B_["b1"], ssc(4, q))
            CP("dve", ssc(6, q), b2(b2.ap[:, T - 1:T]))
            d1, d2, d3, d4 = B_["d"]
            TT("pool", d2, b2, B_["SIN"], ALU.mult)
            TT("pool", d4, b2, B_["COS"], ALU.mult)

        def ssm_B(c, qq):
            q = 4 * c + qq
            py = PS[7]
            sc = scon_r[q % 2]
            d1, d2, d3, d4 = ssmd[4 * (q % 2):4 * (q % 2) + 4]
            cre = sc(sc.ap[:, 256:384])
            ncre = sc(sc.ap[:, 384:512])
            ncim = sc(sc.ap[:, 512:640])
            MM(py, cre, d1, qq == 0, False)
            MM(py, ncre, d2, False, False)
            MM(py, ncim, d3, False, False)
            MM(py, ncim, d4, False, False)
            if qq == 3:
                ssm_chunk_end(c)

        def ssm_chunk_end(c):
            py = PS[7]
            MM(py, diagd(diagd_t[:, c, :]), uT[c], False, True)
            g1, g2 = ssmf[0], ssmf[1]
            ACT(g1, py, AF.Square)
            TS("dve", g1, g1, 0.044715, 1.0, ALU.mult, ALU.add)
            TT("dve", g1, g1, py, ALU.mult)
            ACT(g2, g1, AF.Sigmoid, scale=1.5957691216057308)
            TT("dve", yy[c], g2, py, ALU.mult)
            CP("act", zb[c], yy[c])

        units = []
        for q in range(32):
            units.append(lambda q=q: ssm_F1a(q))
            units.append(lambda q=q: ssm_F1b(q))
            units.append(lambda q=q: ssm_F1c(q))
            units.append(lambda q=q: ssm_F2a(q))
            units.append(lambda q=q: ssm_F2b(q))
            if q >= 1:
                units.append(lambda q=q: ssm_B((q - 1) // 4, (q - 1) % 4))
        units.append(lambda: ssm_B(7, 3))

        nb = 4 * (i + 1)
        items = []
        for h in range(NH):
            for kblk in range(nb - 1, -1, -1):
                items.append((h, kblk))
        st = {}
        kv_slots = {}

        def stage0(n):
            h, kblk = items[n]
            first = kblk == nb - 1
            if first and h not in kv_slots:
                load_kv(h)
            ks, vs = kv_slots[h]
            kk = ks[kblk // 16]
            kb = kk(kk.ap[:, (kblk % 16) * 128:(kblk % 16 + 1) * 128])
            vv = vs[kblk // 16]
            vb = vv(vv.ap[:, (kblk % 16) * 128:(kblk % 16 + 1) * 128])
            pa = PS[n % 2]
            MM(pa, kb, qT[h], True, True)
            e_ = ntf()
            ACT(e_, pa, AF.Exp)
            st[n] = dict(kb=kb, vb=vb, e=e_, dg=kblk - 4 * i, first=first, last=(kblk == 0), h=h)

        def load_kv(h):
            if True:
                nchk = (nb + 15) // 16
                ks, vs = [], []
                for ck in range(nchk):
                    b0, b1 = ck * 16, min(nb, ck * 16 + 16)
                    ksl = nxt("ring", ring)
                    kview = ksl(ksl.ap[:, 0:(b1 - b0) * 128])
                    DMA(kview, V(kcache[h, :, b0 * 128:b1 * 128], [kc_v[h][j].ts[0] for j in range(b0 // 4, (b1 + 3) // 4)]))
                    vsl = nxt("ring", ring)
                    vview = vsl(vsl.ap[:, 0:(b1 - b0) * 128])
                    DMA(vsl(vsl.ap[:, 0:(b1 - b0) * 128].rearrange("p (b d) -> p b d", d=128)),
                        V(vcache[:, h, b0:b1, :], [vc_v[j].ts[0] for j in range(b0 // 4, (b1 + 3) // 4)]))
                    ks.append(kview)
                    vs.append(vview)
                kv_slots[h] = (ks, vs)

        def stage0b(n):
            s_ = st[n]
            spb = ntb()
            ACT(spb, s_["e"], AF.Ln, bias=cb(1))
            dg = s_["dg"]
            if dg >= 0:
                TT("dve", spb, spb, masks(masks_t[:, dg * T:(dg + 1) * T]), ALU.mult)
            s_["spb"] = spb

        def stage1(n):
            s_ = st[n]
            pe_ = PS[2 + n % 2]
            MM(pe_, s_["kb"], qT[s_["h"]], True, False)
            MM(pe_, negtri, s_["spb"], False, s_["first"])
            if not s_["first"]:
                MM(pe_, negones, Sbf[(n + 1) % 2], False, True)
            if not s_["last"]:
                if s_["first"]:
                    CP("dve", Sbf[n % 2], s_["spb"])
                else:
                    TT("dve", Sbf[n % 2], Sbf[(n + 1) % 2], s_["spb"], ALU.add)
            w_ = ntb()
            ACT(w_, pe_, AF.Exp)
            if s_["dg"] >= 0:
                dg = s_["dg"]
                TT("dve", w_, w_, masks(masks_t[:, dg * T:(dg + 1) * T]), ALU.mult)
            s_["w"] = w_

        def stage2(n):
            s_ = st[n]
            h = s_["h"]
            po = PS[4]
            MM(po, s_["vb"], s_["w"], s_["first"], s_["last"])
            if s_["last"]:
                CP("act", hT[8 + h], po)
            del st[n]

        NI = len(items)
        nsteps = NI + 2
        done = 0
        for n in range(nsteps):
            if n < NI:
                stage0(n)
            if 0 <= n - 1 < NI:
                stage1(n - 1)
            if n < NI:
                stage0b(n)
            if 0 <= n - 2 < NI:
                stage2(n - 2)
            if n < NI and n % nb == 2 and (n // nb) + 1 < NH:
                load_kv(n // nb + 1)
            target = ((n + 1) * len(units) + nsteps - 1) // nsteps
            while done < min(target, len(units)):
                units[done]()
                done += 1
        while done < len(units):
            units[done]()
            done += 1

        TT("dve", ss(7), ss(1), ss(5), ALU.mult)
        TT("dve", ss(8), ss(2), ss(6), ALU.mult)
        TT("dve", ss(3), ss(7), ss(8), ALU.subtract)
        TT("dve", ss(7), ss(2), ss(5), ALU.mult)
        TT("dve", ss(8), ss(1), ss(6), ALU.mult)
        TT("dve", ss(4), ss(7), ss(8), ALU.add)
        glb = bidx["ssm_w_glu"]
        for ob in range(4):
            sl_, wv = load_block(glb[ob])
            for o2 in range(2):
                oc = ob * 2 + o2
                pp = PS[oc % 4]
                for kc in range(8):
                    MM(pp, sl_(wv[:, kc, o2 * 128:(o2 + 1) * 128]), zb[kc], kc == 0, kc == 7)
                g = ntf()
                ACT(g, pp, AF.Sigmoid, bias=gcol("b_glu", oc))
                TT("dve", yy[oc], yy[oc], g, ALU.mult)

        dump("yssm", yy, i)
        rs1 = rmsnorm_stats(yy[0:8], 8)
        for c in range(8):
            STT(hT[c], yy[c], gcol("g_ossm", c), rs1, ALU.mult, ALU.mult)
        rs2 = rmsnorm_stats(hT[8:16], 8)
        for c in range(8):
            STT(hT[8 + c], hT[8 + c], gcol("g_osb", c), rs2, ALU.mult, ALU.mult)
        wob = bidx["w_out"]
        for oc in range(KC):
            sl_, wv = load_block(wob[oc])
            pp = PS[oc % 4]
            for kc in range(KC):
                MM(pp, sl_(wv[:, kc, :]), hT[kc], kc == 0, kc == KC - 1)
            TT("dve", xT[oc], pp, xT[oc], ALU.add)

        dump("x2", xT, i)
        P.handoff(mixer_views, actT)
        ffn("ffn2", "g_ffn2")

        dump("x3", xT, i)
        P.handoff(actT, eT)
        norm_to_hT("g_ple")
        pgb = bidx["w_ple_gate"]
        ppb = bidx["w_ple_proj"]
        pslots = []
        for k in range(2):
            dst = tabs_r[k](tabs_t[k][:].bitcast(BF16))
            if i == 0:
                DMA(dst, V(wall_d[ppb[k], :, 0:2048]), eng="pool")
                DMA(wscr_v[ppb[k]](wscr[ppb[k], :, 0:2048]), dst)
            else:
                DMA(dst, wscr_v[ppb[k]](wscr[ppb[k], :, 0:2048]))
            pslots.append((dst, dst.ap.rearrange("p (k n) -> p k n", n=1024)))
        for oc in range(KC):
            sl_, wv = load_block(pgb[oc])
            pg = PS[oc % 2]
            for kc in range(KC):
                MM(pg, sl_(wv[:, kc, :]), hT[kc], kc == 0, kc == KC - 1)
            psl, pwv = pslots[oc // 8]
            pp = PS[2 + oc % 2]
            o8 = oc % 8
            for kc in range(2):
                MM(pp, psl(pwv[:, kc, o8 * 128:(o8 + 1) * 128]), pTb(pTb_t[:, kc, :]), kc == 0, kc == 1)
            g = ntf()
            ACT(g, pg, AF.Sigmoid)
            TT("dve", eT[oc], g, pp, ALU.mult)
        rs = rmsnorm_stats(eT, KC)
        for c in range(KC):
            STT(eT[c], eT[c], gcol("g_post", c), rs, ALU.mult, ALU.mult)
            TT("dve", xT[c], xT[c], eT[c], ALU.add)
        P.handoff(eT, actT)

        for c in range(KC):
            DMA(V(yT_d[c * 128:(c + 1) * 128, cols]), xT[c])

    block = es.enter_context(nc.Block())
    P.emit(nc, block, sems, dsems)
    es.close()
    return nc


def _small_inputs(inp):
    f = np.float32

    def pc(v, n):
        return np.ascontiguousarray(np.asarray(v, f).reshape(n, 128).T)

    out = {}
    out["g_ffn1"] = pc(inp["ffn1_norm"][0], 16)
    out["g_mix"] = pc(inp["mix_norm"][0], 16)
    out["g_ffn2"] = pc(inp["ffn2_norm"][0], 16)
    out["g_ple"] = pc(inp["ple_norm"][0], 16)
    out["g_post"] = pc(inp["ple_post_norm"][0], 16)
    out["g_ossm"] = pc(inp["out_norm_ssm"][0], 8)
    out["g_osb"] = pc(inp["out_norm_sb"][0], 8)
    out["g_q"] = pc(inp["q_norm"][0], 1)
    out["g_k"] = pc(inp["k_norm"][0], 1)
    out["b_glu"] = pc(inp["ssm_b_glu"][0], 8)
    out["d_s"] = pc(inp["ssm_d"][0], 8)
    lre = np.asarray(inp["ssm_lambda_re"][0], f)
    lim = np.asarray(inp["ssm_lambda_im"][0], f)
    ldt = np.repeat(np.asarray(inp["ssm_log_dt"][0], f)[:, None], 64, axis=1)

    def st(a):
        return np.ascontiguousarray(a.reshape(32, 128).T)

    def bc(a):
        return np.ascontiguousarray(np.broadcast_to(a.reshape(1, 4096), (128, 4096)))

    out["lamre_s"], out["lamim_s"], out["logdt_s"] = st(lre), st(lim), st(ldt)
    out["lamre_b"], out["lamim_b"], out["logdt_b"] = bc(lre), bc(lim), bc(ldt)
    bre = np.asarray(inp["ssm_b_re"][0], f)
    bim = np.asarray(inp["ssm_b_im"][0], f)
    cre = np.asarray(inp["ssm_c_re"][0], f)
    cim = np.asarray(inp["ssm_c_im"][0], f)
    bh_re = np.zeros((128, 4096), f)
    bh_im = np.zeros((128, 4096), f)
    cp_re = np.zeros((128, 4096), f)
    cp_im = np.zeros((128, 4096), f)
    for g in range(64):
        q, half = g // 2, g % 2
        rows = slice(16 * (g % 8), 16 * (g % 8) + 16)
        cols_ = slice(q * 128 + 64 * half, q * 128 + 64 * half + 64)
        bh_re[rows, cols_] = bre[g].T
        bh_im[rows, cols_] = bim[g].T
        jr = slice(64 * half, 64 * half + 64)
        cc = slice(q * 128 + 16 * (g % 8), q * 128 + 16 * (g % 8) + 16)
        cp_re[jr, cc] = cre[g].T
        cp_im[jr, cc] = cim[g].T
    out["bh_re"], out["bh_im"], out["cp_re"], out["cp_im"] = bh_re, bh_im, cp_re, cp_im
    s_idx = np.arange(128)[:, None]
    j_idx = np.arange(128)[None, :]
    out["c_negtri"] = np.where(s_idx >= j_idx, -1.0, 0.0).astype(f)
    out["c_ident"] = np.eye(128, dtype=f)
    t_idx = np.arange(512)[None, :]
    out["c_masks"] = np.concatenate(
        [(t_idx > (s_idx + 128 * d)).astype(f) for d in range(4)], axis=1)
    out["c_iota"] = np.ascontiguousarray(np.broadcast_to(np.arange(512, dtype=f)[None, :], (128, 512)))
    return out


def _block_weights(inputs):
    blocks, bidx = make_blocks()
    wall = np.zeros((len(blocks), 128, 2048), np.float32)
    for name, K, N in WNAMES:
        W = np.asarray(inputs[name], np.float32)[0]
        ids = bidx[name]
        if K == D:
            wall[ids[0]:ids[-1] + 1] = W.reshape(16, 128, N // 128, 128).transpose(2, 1, 0, 3).reshape(N // 128, 128, 2048)
        elif K == DFF:
            Wr = W.reshape(FC, 128, N // 128, 128).transpose(2, 1, 0, 3)
            for oc in range(N // 128):
                for s_, (kc0, nk) in enumerate(((0, 16), (16, 16), (32, 12))):
                    wall[ids[oc * 3 + s_], :, 0:nk * 128] = Wr[oc, :, kc0:kc0 + nk, :].reshape(128, nk * 128)
        elif K == 1024:
            wall[ids[0]:ids[-1] + 1] = W.reshape(8, 128, N // 256, 256).transpose(2, 1, 0, 3).reshape(N // 256, 128, 2048)
        elif K == 256:
            wall[ids[0]:ids[-1] + 1] = W.reshape(2, 128, N // 1024, 1024).transpose(2, 1, 0, 3).reshape(N // 1024, 128, 2048)
    return wall


_NC_CACHE = {}


def kernel(**inputs):
    x = np.asarray(inputs["x"], np.float32)
    p = np.asarray(inputs["p"], np.float32)
    B, L, _ = x.shape
    if L not in _NC_CACHE:
        _NC_CACHE[L] = build(L)
    nc = _NC_CACHE[L]
    small = _small_inputs(inputs)
    wts = {"wall": _block_weights(inputs)}
    in_maps = []
    for b in range(B):
        m = {"xT": np.ascontiguousarray(x[b].T), "pT": np.ascontiguousarray(p[0, b].T)}
        m.update(wts)
        m.update(small)
        in_maps.append(m)
    res = run_bass_kernel_spmd(nc, in_maps, core_ids=list(range(B)))
    _LAST["res"] = res.results
    out = np.stack([np.ascontiguousarray(r["yT"].T) for r in res.results], axis=0)
    return out.astype(np.float32)
```
